# Optimizing a Trainium2 kernel written in Bass

```python
import jax, jax.numpy as jnp
from jax import lax
import numpy as np

D_MODEL = 1024
BATCH = 8
SEQ = 4096
DEPTH = 2

D_MIX = D_MODEL
CONV_W = D_MIX // 4
POOL_W = D_MIX // 4
ATTN_W = D_MIX - CONV_W - POOL_W
HEAD_DIM = 64
N_HEADS = ATTN_W // HEAD_DIM
CONV_K = 31
POOL_WINDOWS = (2, 4, 8, 16)
N_POOL_GROUPS = len(POOL_WINDOWS)
POOL_GROUP = POOL_W // N_POOL_GROUPS
GRID_W = 64
WIN_R_MAX = 8
WIN_C = 16
Q_COLS = WIN_C
K_COLS = 2 * WIN_C
D_FF = ((8 * D_MODEL // 3 + 127) // 128) * 128
IN_W = 2 * CONV_W + POOL_W + 3 * ATTN_W
EPS = 1e-6
NEG = -1e30

kernel_name = 'hybrid_conv_pool_natten_macaron_encoder'


def rmsnorm(x, g):
    x32 = x.astype(jnp.float32)
    y = x32 * lax.rsqrt(jnp.mean(x32 * x32, axis=-1, keepdims=True) + EPS)
    return (y * g.astype(jnp.float32)).astype(x.dtype)


def layernorm(x, g, b):
    x32 = x.astype(jnp.float32)
    mu = jnp.mean(x32, axis=-1, keepdims=True)
    var = jnp.mean(jnp.square(x32 - mu), axis=-1, keepdims=True)
    y = (x32 - mu) * lax.rsqrt(var + EPS)
    return (y * g.astype(jnp.float32) + b.astype(jnp.float32)).astype(x.dtype)


def swiglu(h, w_gate, w_up, w_down):
    return (jax.nn.silu(h @ w_gate) * (h @ w_up)) @ w_down


def conv_module(a, gate, dw, dw_b, ln_g, ln_b, pw):
    u = a * jax.nn.sigmoid(gate)
    u = lax.conv_general_dilated(
        u, dw[:, None, :], window_strides=(1,),
        padding=[(CONV_K // 2, CONV_K // 2)],
        dimension_numbers=('NWC', 'WIO', 'NWC'),
        feature_group_count=u.shape[-1]) + dw_b
    u = jax.nn.silu(layernorm(u, ln_g, ln_b))
    return u @ pw


def pool_mixer(p, w_group, scale):
    b, s, _ = p.shape
    pg = p.reshape(b, s, N_POOL_GROUPS, POOL_GROUP)
    csum = jnp.concatenate(
        [jnp.zeros((b, 1, N_POOL_GROUPS, POOL_GROUP), jnp.float32),
         jnp.cumsum(pg.astype(jnp.float32), axis=1)], axis=1)
    t = jnp.arange(s)
    means = []
    for gi, w in enumerate(POOL_WINDOWS):
        lo = jnp.clip(t - w // 2, 0, s)
        hi = jnp.clip(t - w // 2 + w, 0, s)
        win_sum = csum[:, hi, gi] - csum[:, lo, gi]
        means.append(win_sum / (hi - lo).astype(jnp.float32)[:, None])
    pooled = jnp.stack(means, axis=2)
    mixed = (pooled - pg.astype(jnp.float32)).astype(p.dtype)
    y = jnp.einsum('bsgc,gcd->bsgd', mixed, w_group).reshape(b, s, POOL_W)
    return y * scale


def _column_blocks():
    n_cb = GRID_W // Q_COLS
    qcol = np.arange(GRID_W).reshape(n_cb, Q_COLS)
    kstart = np.clip(np.arange(n_cb) * Q_COLS - WIN_C // 2, 0, GRID_W - K_COLS)
    kcol = kstart[:, None] + np.arange(K_COLS)
    c0 = np.clip(qcol - WIN_C // 2, 0, GRID_W - WIN_C)
    valid = (kcol[:, None, :] >= c0[:, :, None]) & (kcol[:, None, :] < c0[:, :, None] + WIN_C)
    dc_idx = np.clip(kcol[:, None, :] - qcol[:, :, None] + WIN_C - 1, 0, 2 * WIN_C - 2)
    return kcol, valid, dc_idx


def neighbourhood_attention(q, k, v, rpb):
    b, s, h, d = q.shape
    rows = s // GRID_W
    kr = min(WIN_R_MAX, rows)
    kcol, valid, dc_idx = _column_blocks()
    n_cb = kcol.shape[0]
    qg = q.reshape(b, rows, n_cb, Q_COLS, h, d).transpose(1, 0, 4, 2, 3, 5)
    kg = k.reshape(b, rows, GRID_W, h, d).transpose(0, 3, 1, 2, 4)
    vg = v.reshape(b, rows, GRID_W, h, d).transpose(0, 3, 1, 2, 4)
    scale = HEAD_DIM ** -0.5
    mask = jnp.asarray(valid)[:, :, None, :]

    def one_row(args):
        q_r, r = args
        r0 = jnp.clip(r - kr // 2, 0, rows - kr)
        k_blk = lax.dynamic_slice_in_dim(kg, r0, kr, axis=2)[:, :, :, kcol]
        v_blk = lax.dynamic_slice_in_dim(vg, r0, kr, axis=2)[:, :, :, kcol]
        sc = jnp.einsum('bhnqd,bhinjd->bhnqij', q_r.astype(jnp.float32),
                        k_blk.astype(jnp.float32)) * scale
        dr_idx = r0 + jnp.arange(kr) - r + WIN_R_MAX - 1
        bias = jnp.take(rpb, dr_idx, axis=1)[:, :, dc_idx]
        bias = bias.transpose(0, 2, 3, 1, 4).astype(jnp.float32)
        sc = jnp.where(mask, sc + bias, NEG)
        p = jax.nn.softmax(sc.reshape(b, h, n_cb, Q_COLS, kr * K_COLS), axis=-1).reshape(sc.shape)
        return jnp.einsum('bhnqij,bhinjd->bhnqd', p.astype(v_blk.dtype), v_blk)

    out = lax.map(one_row, (qg, jnp.arange(rows)))
    return out.transpose(1, 0, 3, 4, 2, 5).reshape(b, s, h * d)


def setup_inputs(seed: int = 0) -> dict:
    key = jax.random.key(seed)
    ks = jax.random.split(key, 24)
    f32 = jnp.float32

    def nrm(k, shape, scale):
        return jax.random.normal(k, shape, f32) * scale

    def gain(k, shape):
        return 1.0 + 0.05 * jax.random.normal(k, shape, f32)

    L = DEPTH
    return {
        'x': jax.random.normal(ks[0], (BATCH, SEQ, D_MODEL), f32),
        'ffn1_norm': gain(ks[1], (L, D_MODEL)),
        'ffn1_gate': nrm(ks[2], (L, D_MODEL, D_FF), D_MODEL ** -0.5),
        'ffn1_up': nrm(ks[3], (L, D_MODEL, D_FF), D_MODEL ** -0.5),
        'ffn1_down': nrm(ks[4], (L, D_FF, D_MODEL), D_FF ** -0.5),
        'mix_norm': gain(ks[5], (L, D_MODEL)),
        'w_in': nrm(ks[6], (L, D_MODEL, IN_W), D_MODEL ** -0.5),
        'conv_dw': nrm(ks[7], (L, CONV_K, CONV_W), CONV_K ** -0.5),
        'conv_dw_b': nrm(ks[8], (L, CONV_W), 0.02),
        'conv_ln_g': gain(ks[9], (L, CONV_W)),
        'conv_ln_b': nrm(ks[10], (L, CONV_W), 0.02),
        'conv_pw': nrm(ks[11], (L, CONV_W, CONV_W), CONV_W ** -0.5),
        'pool_w': nrm(ks[12], (L, N_POOL_GROUPS, POOL_GROUP, POOL_GROUP), POOL_GROUP ** -0.5),
        'pool_scale': gain(ks[13], (L, POOL_W)),
        'q_norm': gain(ks[14], (L, HEAD_DIM)),
        'k_norm': gain(ks[15], (L, HEAD_DIM)),
        'rpb': nrm(ks[16], (L, N_HEADS, 2 * WIN_R_MAX - 1, 2 * WIN_C - 1), 0.1),
        'w_out': nrm(ks[17], (L, D_MIX, D_MODEL), D_MIX ** -0.5),
        'ffn2_norm': gain(ks[18], (L, D_MODEL)),
        'ffn2_gate': nrm(ks[19], (L, D_MODEL, D_FF), D_MODEL ** -0.5),
        'ffn2_up': nrm(ks[20], (L, D_MODEL, D_FF), D_MODEL ** -0.5),
        'ffn2_down': nrm(ks[21], (L, D_FF, D_MODEL), D_FF ** -0.5),
    }


def reference(x, ffn1_norm, ffn1_gate, ffn1_up, ffn1_down, mix_norm, w_in,
              conv_dw, conv_dw_b, conv_ln_g, conv_ln_b, conv_pw, pool_w, pool_scale,
              q_norm, k_norm, rpb, w_out, ffn2_norm, ffn2_gate, ffn2_up, ffn2_down):
    b, s, _ = x.shape
    splits = np.cumsum([CONV_W, CONV_W, POOL_W, ATTN_W, ATTN_W])
    for l in range(DEPTH):
        x = x + 0.5 * swiglu(rmsnorm(x, ffn1_norm[l]), ffn1_gate[l], ffn1_up[l], ffn1_down[l])
        h = rmsnorm(x, mix_norm[l])
        u = h @ w_in[l]
        c_a, c_g, p_in, q, k, v = jnp.split(u, splits, axis=-1)
        c_out = conv_module(c_a, c_g, conv_dw[l], conv_dw_b[l], conv_ln_g[l], conv_ln_b[l], conv_pw[l])
        p_out = pool_mixer(p_in, pool_w[l], pool_scale[l])
        q = rmsnorm(q.reshape(b, s, N_HEADS, HEAD_DIM), q_norm[l])
        k = rmsnorm(k.reshape(b, s, N_HEADS, HEAD_DIM), k_norm[l])
        v = v.reshape(b, s, N_HEADS, HEAD_DIM)
        a_out = neighbourhood_attention(q, k, v, rpb[l])
        x = x + jnp.concatenate([c_out, p_out, a_out], axis=-1) @ w_out[l]
        x = x + 0.5 * swiglu(rmsnorm(x, ffn2_norm[l]), ffn2_gate[l], ffn2_up[l], ffn2_down[l])
    return x
```

```python
import numpy as np
import concourse.bass as bass
import concourse.mybir as mybir
from concourse.bass_utils import run_bass_kernel_spmd


F32 = mybir.dt.float32
BF16 = mybir.dt.bfloat16
AF = mybir.ActivationFunctionType
ALU = mybir.AluOpType

ENGS = ("pe", "act", "dve", "pool", "sp")


class Op:
    __slots__ = ("eng", "fn", "deps", "sig", "sigval", "is_dma", "dsem", "dval", "idx")

    def __init__(self, eng, fn, is_dma=False):
        self.eng = eng
        self.fn = fn
        self.deps = []
        self.sig = False
        self.sigval = 0
        self.is_dma = is_dma
        self.dsem = None
        self.dval = 0


class Prog:
    def __init__(self, nc):
        self.nc = nc
        self.ops = []
        self.last_w = {}
        self.readers = {}
        self.dma_cnt = {}
        self.dma_keys = []
        self.out_ops = []
        self.phase_keys = {}

    def _add(self, op, reads, writes):
        deps = set()
        for r in reads:
            w = self.last_w.get(r)
            if w is not None:
                deps.add(w)
        for r in writes:
            w = self.last_w.get(r)
            if w is not None:
                deps.add(w)
            lastc = {}
            for rd in self.readers.get(r, ()):
                if rd.is_dma:
                    deps.add(rd)
                else:
                    lastc[rd.eng] = rd
            deps.update(lastc.values())
        deps.discard(op)
        raw = set(self.last_w.get(r) for r in reads)
        for d in deps:
            if (d not in raw) and (not d.is_dma) and (not op.is_dma) and d.eng == op.eng and d.eng == "pe":
                continue
            op.deps.append(d)
        for r in reads:
            self.readers.setdefault(r, []).append(op)
        for r in writes:
            self.last_w[r] = op
            self.readers[r] = []
        op.idx = len(self.ops)
        self.ops.append(op)
        return op

    def op(self, eng, fn, reads=(), writes=()):
        return self._add(Op(eng, fn), reads, writes)

    def dma(self, eng, out, in_, key, reads=(), writes=(), **kw):
        def fn(e, out=out, in_=in_, kw=kw):
            return e.dma_start(out=out, in_=in_, **kw)
        o = Op(eng, fn, is_dma=True)
        cls = "sw" if eng == "pool" else "hw"
        if key not in self.phase_keys:
            self.phase_keys[key] = (cls, sum(1 for v in self.phase_keys.values() if v[0] == cls))
        slot = self.phase_keys[key]
        assert slot[0] == cls, "a DMA key must stay on one queue class"
        if slot not in self.dma_cnt:
            self.dma_cnt[slot] = 0
            self.dma_keys.append(slot)
        self.dma_cnt[slot] += 16
        o.dsem = slot
        o.dval = self.dma_cnt[slot]
        return self._add(o, reads, writes)

    def emit(self, final_wait_ops=()):
        nc = self.nc
        raw_same = set()
        for o in self.ops:
            for d in o.deps:
                if not d.is_dma:
                    d.sig = True
        cnt = {e: 0 for e in ENGS}
        for o in self.ops:
            if not o.is_dma and o.sig:
                cnt[o.eng] += 1
                o.sigval = cnt[o.eng]
        self.maxsig = dict(cnt)
        per_eng = {e: [o for o in self.ops if o.eng == e] for e in ENGS}
        import contextlib
        with contextlib.ExitStack() as st:
            esem = {e: st.enter_context(nc.semaphore("s_" + e)) for e in ENGS}
            dsem = {k: st.enter_context(nc.semaphore("d_%s%d" % k)) for k in self.dma_keys}
            block = st.enter_context(nc.Block())

            def body(ename, e):
                known = {}
                for o in per_eng[ename]:
                    need = {}
                    for d in o.deps:
                        if d.is_dma:
                            k = ("d", d.dsem)
                            v = d.dval
                        else:
                            if d.eng == ename:
                                pass
                            k = ("e", d.eng)
                            v = d.sigval
                        if known.get(k, 0) >= v:
                            continue
                        if need.get(k, 0) < v:
                            need[k] = v
                    for k, v in need.items():
                        sem = dsem[k[1]] if k[0] == "d" else esem[k[1]]
                        e.wait_ge(sem, v)
                        known[k] = v
                    if o.fn is None:
                        continue
                    ins = o.fn(e)
                    if o.is_dma:
                        ins.then_inc(dsem[o.dsem], 16)
                    elif o.sig:
                        ins.then_inc(esem[ename], 1)
                if ename == "sp":
                    for o in final_wait_ops:
                        e.wait_ge(dsem[o.dsem], o.dval)

            @block.tensor
            def _(e):
                body("pe", e)

            @block.scalar
            def _(e):
                body("act", e)

            @block.vector
            def _(e):
                body("dve", e)

            @block.gpsimd
            def _(e):
                body("pool", e)

            @block.sync
            def _(e):
                body("sp", e)


def _barrier(self):
    last = {}
    for o in self.ops:
        if o.fn is None:
            continue
        if o.is_dma:
            last[("d", o.dsem)] = o
        else:
            last[("e", o.eng)] = o
    deps = list(last.values())
    for e in ENGS:
        b = Op(e, None)
        b.deps = list(deps)
        b.idx = len(self.ops)
        self.ops.append(b)
    self.last_w = {}
    self.readers = {}
    self.phase_keys = {}


Prog.barrier = _barrier


D = 1024
DFF = 2816
NKC = D // 128
NFC = DFF // 128
EPS = 1e-6


def bcast_rows(ap1d, n, parts=128):
    return bass.AP(ap1d.tensor, ap1d.offset, [[0, parts], [1, n]])


class Consts:
    def __init__(self, nc, st):
        self.ident = st.enter_context(nc.sbuf_tensor("ident", [128, 128], BF16))
        self.cneg = st.enter_context(nc.sbuf_tensor("cneg", [128, 1], F32))
        self.ceps = st.enter_context(nc.sbuf_tensor("ceps", [128, 1], F32))
        sb = lambda n, shape, dt: st.enter_context(nc.sbuf_tensor("G" + n, shape, dt))
        self.gb = sb("gb", [128, D], F32)
        self.xt = [sb("xt%d" % i, [128, D], F32) for i in range(2)]
        self.hb = [sb("hb%d" % i, [128, D], BF16) for i in range(2)]
        self.ss = [sb("ss%d" % i, [128, 1], F32) for i in range(2)]
        self.ms = [sb("ms%d" % i, [128, 1], F32) for i in range(2)]
        self.rs = [sb("rs%d" % i, [128, 1], F32) for i in range(2)]
        self.hT0 = sb("hT0", [128, NKC, 512], BF16)
        self.W0 = sb("W0", [128, 4096], BF16)

    def init(self, P):
        ident, cneg, ceps = self.ident, self.cneg, self.ceps
        P.op("pool", lambda e: e.memset(ceps[:], EPS), writes=["ceps"])
        P.op("pool", lambda e: e.memset(ident[:], 0.0), writes=["ident"])
        P.op("pool", lambda e: e.affine_select(out=ident[:], in_=ident[:], pattern=[[-1, 128]],
                                                compare_op=ALU.not_equal, fill=1.0, base=0,
                                                channel_multiplier=1), reads=["ident"], writes=["ident"])
        P.op("pool", lambda e: e.memset(cneg[:], -0.5), writes=["cneg"])


def rms_rstd(P, tag, ss, ms, rstd, cneg, n):
    P.op("pool", lambda e: e.tensor_scalar(out=ms, in0=ss, scalar1=1.0 / n, scalar2=EPS,
                                           op0=ALU.mult, op1=ALU.add),
         reads=[("ss", tag)], writes=[("ms", tag)])
    P.op("pool", lambda e: e.tensor_tensor(out=rstd, in0=ms, in1=cneg, op=ALU.pow),
         reads=[("ms", tag), "cneg"], writes=[("rstd", tag)])


def ffn_phase(P, nc, st, C, T, x_in, x_out, g_ap, wg, wu, wd, ps, TT=1024, GRP=2, name="f", xout_name="x", next_pre=None):
    nblk = TT // 128
    nsub = TT // 512
    npass = T // TT
    ngrp = NFC // GRP
    sb = lambda n, shape, dt: st.enter_context(nc.sbuf_tensor(name + n, shape, dt))
    assert TT == 1024 and GRP == 2
    gb, xt, hb, st_ss, st_ms, st_rs = C.gb, C.xt, C.hb, C.ss, C.ms, C.rs
    xr = [sb("xr%d" % i, [128, D], F32) for i in range(2)]
    hTs = [C.hT0] + [sb("hT%d" % i, [128, NKC, 512], BF16) for i in range(1, nsub)]
    wgu = [C.W0[:, 0:NKC * 2 * GRP * 128].rearrange("p (k w c) -> p k w c", k=NKC, w=2),
           sb("wgu1", [128, NKC, 2, GRP * 128], BF16), sb("wgu2", [128, NKC, 2, GRP * 128], BF16)]
    WRES = [("G", "W0"), (name, "wgu", 1), (name, "wgu", 2)]
    aT = sb("aT", [128, NFC, TT], BF16)
    wdb = sb("wdb", [128, NFC, D], BF16)
    sg = [sb("sg%d" % i, [128, 512], F32) for i in range(2)]
    ident = C.ident

    pT = [ps[0][:].bitcast(BF16), ps[1][:].bitcast(BF16)]
    wgv = wg.rearrange("(kc p) n -> p kc n", p=128)
    wuv = wu.rearrange("(kc p) n -> p kc n", p=128)
    wdv = wd.rearrange("(m p) n -> p m n", p=128)

    def load_wdb(i):
        P.dma("pool", wdb[:, 2 * i:2 * i + 2, :], wdv[:, 2 * i:2 * i + 2, :],
              key=(name, "wdb", i), writes=[(name, "wdb", i)])

    def load_wgu(p, g):
        slot = (p * ngrp + g) % 3
        c0 = g * GRP * 128
        P.dma("pool", wgu[slot][:, :, 0, :], wgv[:, :, c0:c0 + GRP * 128], key=WRES[slot],
              writes=[WRES[slot]])
        P.dma("pool", wgu[slot][:, :, 1, :], wuv[:, :, c0:c0 + GRP * 128], key=WRES[slot],
              writes=[WRES[slot]])

    ybank = [0]
    HRES = [("G", "hT0")] + [(name, "hT", i) for i in range(1, nsub)]

    def a1(p, b):
        t0 = p * TT
        i = b % 2
        P.dma("sp", xt[i][:], x_in[t0 + b * 128:t0 + (b + 1) * 128, :], key=("G", "xt", i),
              writes=[("G", "xt", i)])
        P.op("dve", lambda e, i=i: e.scalar_tensor_tensor(out=hb[i][:], in0=xt[i][:], scalar=1.0, in1=xt[i][:],
                                                         op0=ALU.mult, op1=ALU.mult, accum_out=st_ss[i][:]),
             reads=[("G", "xt", i)], writes=[("G", "hb", i), ("ss", ("G", i))])
        rms_rstd(P, ("G", i), st_ss[i][:], st_ms[i][:], st_rs[i][:], C.cneg[:], D)

    def a2(p, b):
        i = b % 2
        P.op("dve", lambda e, i=i: e.scalar_tensor_tensor(out=hb[i][:], in0=xt[i][:], scalar=st_rs[i][:], in1=gb[:],
                                                         op0=ALU.mult, op1=ALU.mult),
             reads=[("G", "xt", i), ("rstd", ("G", i)), ("G", "gb")], writes=[("G", "hb", i)])

    def a_back(p, b):
        i = b % 2
        for kc in range(NKC):
            P.op("pe", lambda e, i=i, kc=kc: e.transpose(out=pT[i][:, kc * 128:(kc + 1) * 128],
                                                       in_=hb[i][:, kc * 128:(kc + 1) * 128], identity=ident[:]),
                 reads=[("G", "hb", i), "ident"], writes=[("ps", i)])
        P.op("act", lambda e, i=i, b=b: e.copy(out=hTs[b // 4][:, :, (b % 4) * 128:(b % 4 + 1) * 128],
                                              in_=pT[i].rearrange("p (k t) -> p k t", k=NKC)),
             reads=[("ps", i)], writes=[HRES[b // 4]])

    def c_mm(p, b):
        t0 = p * TT
        i = b % 2
        P.dma("sp", xr[i][:], x_in[t0 + b * 128:t0 + (b + 1) * 128, :], key=(name, "xr", i),
              writes=[(name, "xr", i)])
        ybs = []
        for hf in range(2):
            yb = (6, 7, 2, 3, 4, 5)[(2 * b + hf - (2 * nblk - 2)) % 6]
            ybs.append(yb)
            Y = ps[yb]
            for m in range(NFC):
                P.op("pe", lambda e, m=m, b=b, hf=hf, Y=Y: e.matmul(
                    Y[:], lhsT=aT[:, m, b * 128:(b + 1) * 128], rhs=wdb[:, m, hf * 512:(hf + 1) * 512],
                    start=(m == 0), stop=(m == NFC - 1)),
                    reads=[(name, "aT", b // 4), (name, "wdb", m // 2)], writes=[("ps", yb)])
        return ybs

    def c_evac(p, b, ybs):
        t0 = p * TT
        i = b % 2
        for hf in range(2):
            yb = ybs[hf]
            Y = ps[yb]
            P.op("dve", lambda e, Y=Y, i=i, hf=hf: e.scalar_tensor_tensor(
                out=xr[i][:, hf * 512:(hf + 1) * 512], in0=Y[:], scalar=0.5, in1=xr[i][:, hf * 512:(hf + 1) * 512],
                op0=ALU.mult, op1=ALU.add),
                reads=[("ps", yb), (name, "xr", i)], writes=[(name, "xr", i)])
        o = P.dma("sp", x_out[t0 + b * 128:t0 + (b + 1) * 128, :], xr[i][:], key=(name, "xo", i),
                  reads=[(name, "xr", i)], writes=[("xd", xout_name, (t0 // 128) + b)])
        P.out_ops.append(o)

    for b in (4, 5):
        a1(0, b)
        a2(0, b)
    pro = [[lambda: a_back(0, 4), lambda: a_back(0, 5), lambda: (a1(0, 6), a2(0, 6)), lambda: (a1(0, 7), a2(0, 7))],
           [lambda: a_back(0, 6), lambda: a_back(0, 7)]]
    wloaded = {0}

    def gate_up(p, g, s):
        q = p * ngrp + g
        slot = q % 3
        for c in range(GRP):
            m = g * GRP + c
            par = (s * GRP + c) % 2
            G = ps[2 + 2 * par]
            U = ps[3 + 2 * par]
            for which, dst in ((0, G), (1, U)):
                for kc in range(NKC):
                    P.op("pe", lambda e, kc=kc, which=which, dst=dst, slot=slot, c=c, s=s: e.matmul(
                        dst[:], lhsT=wgu[slot][:, kc, which, c * 128:(c + 1) * 128],
                        rhs=hTs[s][:, kc, :], start=(kc == 0), stop=(kc == NKC - 1)),
                        reads=[WRES[slot], HRES[s]], writes=[("ps", 2 + 2 * par + which)])
            P.op("act", lambda e, G=G, par=par: e.activation(out=sg[par][:], in_=G[:], func=AF.Silu),
                 reads=[("ps", 2 + 2 * par)], writes=[(name, "sg", par)])
            P.op("dve", lambda e, U=U, par=par, m=m, s=s: e.tensor_tensor(
                out=aT[:, m, s * 512:(s + 1) * 512], in0=sg[par][:], in1=U[:], op=ALU.mult),
                reads=[(name, "sg", par), ("ps", 3 + 2 * par)], writes=[(name, "aT", s)])

    def prefetch(q):
        for q2 in (q + 1, q + 2):
            if q2 < npass * ngrp and q2 not in wloaded:
                wloaded.add(q2)
                load_wgu(q2 // ngrp, q2 % ngrp)

    for p in range(npass):
        if p == 0:
            prefetch(0)
            for i in range(0, 2):
                load_wdb(i)
            gate_up(0, 0, 0)
            for fn_ in pro.pop(0):
                fn_()
            gate_up(0, 1, 0)
            for fn_ in pro.pop(0):
                fn_()
            gate_up(0, 0, 1)
            gate_up(0, 1, 1)
            g_start = 2
        else:
            g_start = 0
        for g in range(g_start, ngrp):
            prefetch(p * ngrp + g)
            if p == 0:
                for i in range(g * 11 // ngrp, (g + 1) * 11 // ngrp):
                    load_wdb(i)
            for s in range(nsub):
                gate_up(p, g, s)
        nxt = p + 1 < npass
        if nxt:
            a1(p + 1, 0)
            a2(p + 1, 0)
        elif next_pre is not None and npass > 1:
            next_pre[0]()
        for b in range(nblk):
            ybs = c_mm(p, b)
            if nxt:
                a_back(p + 1, b)
                if b + 1 < nblk:
                    a1(p + 1, b + 1)
            c_evac(p, b, ybs)
            if nxt and b + 1 < nblk:
                a2(p + 1, b + 1)
            if next_pre is not None and not nxt:
                if npass == 1:
                    for k_, bb in enumerate((3, 5, 7)):
                        if b == bb:
                            next_pre[k_]()
                else:
                    for k_, bb in ((1, 3), (2, 6)):
                        if b == bb:
                            next_pre[k_]()


CONV_K = 31
POOL_WINDOWS = (2, 4, 8, 16)


class Banks:
    def __init__(self, ps, lo=2, hi=8):
        self.ps, self.lo, self.n, self.i = ps, lo, hi - lo, 0

    def next(self):
        k = self.lo + (self.i % self.n)
        self.i += 1
        return self.ps[k], ("ps", k)


def norm_front1(P, nc, name, i, x_src, xt, hb, st_ss, st_ms, st_rs, gb, C, xres=None, q="sp"):
    P.dma(q, xt[i][:], x_src, key=(name, "xt", i), reads=([xres] if xres else []), writes=[(name, "xt", i)])
    P.op("dve", lambda e: e.scalar_tensor_tensor(out=hb[i][:], in0=xt[i][:], scalar=1.0, in1=xt[i][:],
                                                 op0=ALU.mult, op1=ALU.mult, accum_out=st_ss[i][:]),
         reads=[(name, "xt", i)], writes=[(name, "hb", i), ("ss", (name, i))])
    rms_rstd(P, (name, i), st_ss[i][:], st_ms[i][:], st_rs[i][:], C.cneg[:], D)


def norm_front2(P, nc, name, i, xt, hb, st_rs, gb, C):
    P.op("dve", lambda e: e.scalar_tensor_tensor(out=hb[i][:], in0=xt[i][:], scalar=st_rs[i][:], in1=gb[:],
                                                 op0=ALU.mult, op1=ALU.mult),
         reads=[(name, "xt", i), ("rstd", (name, i)), (name, "gb")], writes=[(name, "hb", i)])


def norm_front(P, nc, name, i, x_src, xt, hb, st_ss, st_ms, st_rs, gb, C, xres=None):
    norm_front1(P, nc, name, i, x_src, xt, hb, st_ss, st_ms, st_rs, gb, C, xres=xres)
    norm_front2(P, nc, name, i, xt, hb, st_rs, gb, C)


def norm_back(P, nc, name, i, hb, C, pT, psres):
    for kc in range(NKC):
        P.op("pe", lambda e, kc=kc: e.transpose(out=pT[:, kc * 128:(kc + 1) * 128],
                                                in_=hb[i][:, kc * 128:(kc + 1) * 128], identity=C.ident[:]),
             reads=[(name, "hb", i), "ident"], writes=[psres])


def norm_block(P, nc, name, i, x_src, xt, hb, st_ss, st_ms, st_rs, gb, C, pT, psres):
    norm_front(P, nc, name, i, x_src, xt, hb, st_ss, st_ms, st_rs, gb, C)
    norm_back(P, nc, name, i, hb, C, pT, psres)


def preamble(P, nc, C, ps, x_in, xin_name, g_ap, w0_loads):
    pT = [ps[0][:].bitcast(BF16), ps[1][:].bitcast(BF16)]

    def front(bs):
        for b in bs:
            norm_front1(P, nc, "G", b % 2, x_in[b * 128:(b + 1) * 128, :], C.xt, C.hb, C.ss, C.ms, C.rs, C.gb, C,
                        xres=("xd", xin_name, b), q="act")
        for b in bs:
            norm_front2(P, nc, "G", b % 2, C.xt, C.hb, C.rs, C.gb, C)

    def back(bs):
        for b in bs:
            i = b % 2
            norm_back(P, nc, "G", i, C.hb, C, pT[i], ("ps", i))
            P.op("act", lambda e, i=i, b=b: e.copy(out=C.hT0[:, :, b * 128:(b + 1) * 128],
                                                  in_=pT[i].rearrange("p (k t) -> p k t", k=NKC)),
                 reads=[("ps", i)], writes=[("G", "hT0")])

    def s0():
        P.dma("act", C.gb[:], bcast_rows(g_ap, D), key=("G", "gb"), writes=[("G", "gb")])
        for (dst, src) in w0_loads:
            P.dma("pool", dst, src, key=("G", "W0"), writes=[("G", "W0")])
        front((0, 1))

    def s1():
        back((0, 1))
        front((2, 3))

    def s2():
        back((2, 3))
    return [s0, s1, s2]


def mixer_a(P, nc, st, C, T, x_in, x_mid, W, ps, name="ma", xout_name="xm", next_pre=None):
    nsub = T // 512
    sb = lambda n, shape, dt: st.enter_context(nc.sbuf_tensor(name + n, shape, dt))
    gb = C.gb
    xt = C.xt + [sb("xt%d" % i, [128, D], F32) for i in (2, 3)]
    hb = C.hb + [sb("hb%d" % i, [128, D], BF16) for i in (2, 3)]
    st_ss = C.ss + [sb("ss%d" % i, [128, 1], F32) for i in (2, 3)]
    st_ms = C.ms + [sb("ms%d" % i, [128, 1], F32) for i in (2, 3)]
    st_rs = C.rs + [sb("rs%d" % i, [128, 1], F32) for i in (2, 3)]
    xr = [sb("xr%d" % i, [128, D], F32) for i in range(2)]
    hT = [C.hT0, sb("hT1", [128, NKC, 512], BF16)]
    HRES = [("G", "hT0"), (name, "hT", 1)]
    wina = C.W0[:, 0:NKC * 512].rearrange("p (k n) -> p k n", k=NKC)
    winp = sb("winp", [128, NKC, 256], BF16)
    glu = [sb("glu%d" % i, [128, T + 32], BF16) for i in range(2)]
    PA = [sb("PA%d" % i, [128, T + 16], F32) for i in range(2)]
    F = [sb("F%d" % i, [128, 528], F32) for i in range(4)]
    mixed = [sb("mx%d" % i, [128, 512], BF16) for i in range(4)]
    Dg = sb("Dg", [128, 2, CONV_K, 128], BF16)
    poT = [sb("poT%d" % i, [128, T], BF16) for i in range(2)]
    sgm = [sb("sgm%d" % i, [128, 512], F32) for i in range(2)]
    dwT = sb("dwT", [128, 2, CONV_K], F32)
    dwb = sb("dwb", [128, 2], F32)
    lng = sb("lng", [128, 2], F32)
    lnb = sb("lnb", [128, 2], F32)
    psc = sb("psc", [128, 2], F32)
    pwb = sb("pwb", [128, 2, 256], BF16)
    wblk = sb("wblk", [128, 2, 128], BF16)
    woa = sb("woa", [128, 4, D], BF16)
    onesm = sb("onesm", [128, 128], F32)
    rcL = sb("rcL", [128, 2, 8], F32)
    rcR = sb("rcR", [128, 2, 8], F32)
    tmpE = sb("tmpE", [128, 16], F32)
    co2 = [[sb("co%d_%d" % (q, i), [128, 512], F32) for i in range(2)] for q in range(2)]
    sq = [sb("sq%d" % i, [128, 512], F32) for i in range(2)]
    m2 = sb("m2", [128, 512], F32)
    var = sb("var", [128, 512], F32)
    rstd = sb("rstd", [128, 512], F32)
    t1 = [sb("t1%d" % i, [128, 512], F32) for i in range(2)]
    actb = [sb("act%d" % i, [128, 512], BF16) for i in range(2)]
    coT = sb("coT", [128, 2, 512], BF16)

    pT = [ps[0][:].bitcast(BF16), ps[1][:].bitcast(BF16)]
    bk = Banks(ps)

    P.dma("pool", winp[:], W["w_in"].rearrange("(kc p) n -> p kc n", p=128)[:, :, 512:768], key=(name, "winp"),
          writes=[(name, "winp")])
    dwraw = sb("dwraw", [CONV_K, 256], BF16)
    P.dma("pool", dwraw[:], W["conv_dw"], key=(name, "dwraw"), writes=[(name, "dwraw")])
    for cc in range(2):
        for tl, src in ((dwb, "conv_dw_b"), (lng, "conv_ln_g"), (lnb, "conv_ln_b"), (psc, "pool_scale")):
            a = W[src]
            P.dma("sp", tl[:, cc:cc + 1], bass.AP(a.tensor, a.offset + cc * 128, [[1, 128], [1, 1]]),
                  key=(name, "par", src, cc), writes=[(name, "par", src, cc)])
    PARS = [(name, "par", src, cc) for src in ("conv_dw_b", "conv_ln_g", "conv_ln_b", "pool_scale") for cc in range(2)]
    P.dma("pool", pwb[:], W["conv_pw"].rearrange("(cc p) n -> p cc n", p=128), key=(name, "par2"),
          writes=[(name, "par2")])
    P.op("dve", lambda e: e.memset(wblk[:], 0.0), writes=[(name, "wblk")])
    pw_ = W["pool_w"]
    for g in range(4):
        pc, h = g // 2, g % 2
        P.dma("pool", wblk[h * 64:(h + 1) * 64, pc, h * 64:(h + 1) * 64], pw_[g], key=(name, "wblk"),
              reads=[(name, "wblk")], writes=[(name, "wblk", g)])
    WBLK = [(name, "wblk", g) for g in range(4)]
    wov = W["w_out"].rearrange("(kc p) n -> p kc n", p=128)
    P.dma("pool", woa[:], wov[:, 0:4, :], key=(name, "woa"), writes=[(name, "woa")])
    P.op("dve", lambda e: e.memset(onesm[:], 1.0 / 256.0), writes=[(name, "onesm")])
    for g, w in enumerate(POOL_WINDOWS):
        pc, h = g // 2, g % 2
        rows = slice(h * 64, (h + 1) * 64)
        P.op("dve", lambda e, rows=rows, pc=pc, w=w: e.memset(rcL[rows, pc, :], 1.0 / w), writes=[(name, "rc")])
        P.op("dve", lambda e, rows=rows, pc=pc, w=w: e.memset(rcR[rows, pc, :], 1.0 / w), writes=[(name, "rc")])
        for t in range(8):
            cl = min(w, t + w // 2)
            if cl != w:
                P.op("dve", lambda e, rows=rows, pc=pc, t=t, cl=cl: e.memset(rcL[rows, pc, t:t + 1], 1.0 / cl),
                     writes=[(name, "rc")])
            k = 8 - t
            cr = min(w, w // 2 + k)
            if cr != w:
                P.op("dve", lambda e, rows=rows, pc=pc, t=t, cr=cr: e.memset(rcR[rows, pc, t:t + 1], 1.0 / cr),
                     writes=[(name, "rc")])
    for i in range(2):
        P.op("pool", lambda e, i=i: e.memset(glu[i][:, 0:16], 0.0), writes=[(name, "glu", i)])
        P.op("pool", lambda e, i=i: e.memset(glu[i][:, 16 + T:32 + T], 0.0), writes=[(name, "glu", i)])
        P.op("pool", lambda e, i=i: e.memset(PA[i][:, 0:8], 0.0), writes=[(name, "PA", i)])
        P.op("pool", lambda e, i=i: e.memset(PA[i][:, 8 + T:16 + T], 0.0), writes=[(name, "PA", i)])

    def nfront(b):
        i = b % 4
        norm_front(P, nc, "G", i, x_in[b * 128:(b + 1) * 128, :], xt, hb, st_ss, st_ms, st_rs, gb, C)

    def nback(b):
        i, ip, b4, hs_ = b % 4, b % 2, b % 4, (b // 4) % 2
        norm_back(P, nc, "G", i, hb, C, pT[ip], ("ps", ip))
        P.op("act", lambda e: e.copy(out=hT[hs_][:, :, b4 * 128:(b4 + 1) * 128],
                                     in_=pT[ip].rearrange("p (k t) -> p k t", k=NKC)),
             reads=[("ps", ip)], writes=[HRES[hs_]])

    for cc in range(2):
        bank, res = bk.next()
        tb = bank[:].bitcast(BF16)
        P.op("pe", lambda e, cc=cc, tb=tb: e.transpose(out=tb[:, 0:CONV_K], in_=dwraw[:, cc * 128:(cc + 1) * 128],
                                                      identity=C.ident[0:CONV_K, 0:CONV_K]),
             reads=[(name, "dwraw"), "ident"], writes=[res])
        P.op("act", lambda e, cc=cc, tb=tb: e.copy(out=dwT[:, cc, :], in_=tb[:, 0:CONV_K]), reads=[res], writes=[(name, "dwT")])
    for cc in range(2):
        for k in range(CONV_K):
            P.op("dve", lambda e, cc=cc, k=k: e.tensor_scalar(out=Dg[:, cc, k, :], in0=C.ident[:],
                                                             scalar1=dwT[:, cc, k:k + 1], scalar2=None, op0=ALU.mult),
                 reads=["ident", (name, "dwT")], writes=[(name, "Dg")])
    for s in range(nsub):
        hs = s % 2
        nb0 = (s + 1) * 4
        more = s + 1 < nsub

        def proj(col0):
            bank, res = bk.next()
            for kc in range(NKC):
                wsrc, wres, cc0 = (wina, ("G", "W0"), col0) if col0 < 512 else (winp, (name, "winp"), col0 - 512)
                P.op("pe", lambda e, kc=kc, bank=bank, cc0=cc0, hs=hs, wsrc=wsrc: e.matmul(
                    bank[:], lhsT=wsrc[:, kc, cc0:cc0 + 128], rhs=hT[hs][:, kc, :],
                    start=(kc == 0), stop=(kc == NKC - 1)),
                    reads=[wres, HRES[hs]], writes=[res])
            return bank, res
        SCH = {2: [("b", 0)], 3: [("b", 1)], 4: [("b", 2)], 5: [("b", 3)]}
        tcount = [0]

        def hook():
            if more:
                for kind, bb in SCH.get(tcount[0], ()):
                    (nfront if kind == "f" else nback)(nb0 + bb)
            tcount[0] += 1
        if more:
            for bb in range(4):
                norm_front1(P, nc, "G", bb, x_in[(nb0 + bb) * 128:(nb0 + bb + 1) * 128, :], xt, hb, st_ss, st_ms, st_rs, gb, C)
            for bb in range(4):
                norm_front2(P, nc, "G", bb, xt, hb, st_rs, gb, C)
        for cc in range(2):
            A, ra = proj(cc * 128)
            hook()
            G, rg = proj(256 + cc * 128)
            P.op("act", lambda e, G=G, cc=cc: e.activation(out=sgm[cc][:], in_=G[:], func=AF.Sigmoid),
                 reads=[rg], writes=[(name, "sgm", cc)])
            P.op("dve", lambda e, A=A, cc=cc, s=s: e.tensor_tensor(out=glu[cc][:, 16 + s * 512:16 + (s + 1) * 512],
                                                                  in0=sgm[cc][:], in1=A[:], op=ALU.mult),
                 reads=[ra, (name, "sgm", cc)], writes=[(name, "glu", cc)])
            hook()
        for pc in range(2):
            Pp, rp = proj(512 + pc * 128)
            P.op("act", lambda e, Pp=Pp, pc=pc, s=s: e.copy(out=PA[pc][:, 8 + s * 512:8 + (s + 1) * 512], in_=Pp[:]),
                 reads=[rp], writes=[(name, "PA", pc)])
            hook()

    npdone = 0
    if next_pre is not None:
        next_pre[0]()
    rot = [0]

    def rbank():
        k = (0, 1, 6, 7)[rot[0] % 4]
        rot[0] += 1
        return ps[k], ("ps", k)

    def pool_dve(s):
        a = s * 512
        par = s % 2
        for pc in range(2):
            p = PA[pc]
            rp = (name, "PA", pc)
            F0, F1 = F[2 * par], F[2 * par + 1]
            r0, r1 = (name, "F", 2 * par), (name, "F", 2 * par + 1)
            mx = mixed[2 * pc + par]
            rmx = (name, "mx", 2 * pc + par)
            P.op("pool", lambda e, p=p, F0=F0: e.tensor_tensor(out=F0[:, 1:528], in0=p[:, a:a + 527], in1=p[:, a + 1:a + 528], op=ALU.add),
                 reads=[rp], writes=[r0])
            P.op("pool", lambda e, F0=F0, F1=F1: e.tensor_tensor(out=F1[:, 2:526], in0=F0[:, 1:525], in1=F0[:, 3:527], op=ALU.add),
                 reads=[r0], writes=[r1])
            if pc == 1:
                P.op("pool", lambda e, F0=F0, F1=F1: e.tensor_tensor(out=F0[:, 4:524], in0=F1[:, 2:522], in1=F1[:, 6:526], op=ALU.add),
                     reads=[r1], writes=[r0])
                P.op("pool", lambda e, F0=F0, F1=F1: e.tensor_tensor(out=F1[:, 8:520], in0=F0[:, 4:516], in1=F0[:, 12:524], op=ALU.add),
                     reads=[r0], writes=[r1])
            for h in range(2):
                w = POOL_WINDOWS[pc * 2 + h]
                S, rS = (F0, r0) if h == 0 else (F1, r1)
                rows = slice(h * 64, (h + 1) * 64)
                P.op("dve", lambda e, S=S, rows=rows, w=w, p=p, mx=mx: e.scalar_tensor_tensor(
                    out=mx[rows, :], in0=S[rows, 8:520], scalar=1.0 / w, in1=p[rows, 8 + a:8 + a + 512],
                    op0=ALU.mult, op1=ALU.subtract),
                    reads=[rS, rp], writes=[rmx])
                edges = []
                if s == 0:
                    edges.append((0, rcL, 0))
                if s == nsub - 1:
                    edges.append((504, rcR, 8))
                for (c0, rc, e0) in edges:
                    P.op("dve", lambda e, S=S, rows=rows, c0=c0, rc=rc, e0=e0, pc=pc: e.tensor_tensor(
                        out=tmpE[rows, e0:e0 + 8], in0=S[rows, 8 + c0:16 + c0], in1=rc[rows, pc, :], op=ALU.mult),
                        reads=[rS, (name, "rc")], writes=[(name, "tmpE")])
                    P.op("dve", lambda e, rows=rows, c0=c0, e0=e0, p=p, mx=mx: e.tensor_tensor(
                        out=mx[rows, c0:c0 + 8], in0=tmpE[rows, e0:e0 + 8], in1=p[rows, 8 + a + c0:16 + a + c0], op=ALU.subtract),
                        reads=[(name, "tmpE"), rp], writes=[rmx])

    def pool_mm(s):
        par = s % 2
        for pc in range(2):
            bank, res = rbank()
            mx = mixed[2 * pc + par]
            P.op("pe", lambda e, bank=bank, pc=pc, mx=mx: e.matmul(bank[:], lhsT=wblk[:, pc, :], rhs=mx[:], start=True, stop=True),
                 reads=WBLK + [(name, "mx", 2 * pc + par)], writes=[res])
            P.op("dve", lambda e, bank=bank, pc=pc: e.tensor_scalar(
                out=poT[pc][:, s * 512:(s + 1) * 512], in0=bank[:], scalar1=psc[:, pc:pc + 1], scalar2=None, op0=ALU.mult),
                reads=[res] + PARS, writes=[(name, "poT", pc)])

    F32R = mybir.dt.float32r

    def conv_mm(s):
        co = co2[s % 2]
        for cc in range(2):
            bank, res = ps[2 + cc], ("ps", 2 + cc)
            for k in range(CONV_K):
                P.op("pe", lambda e, bank=bank, cc=cc, k=k: e.matmul(
                    bank[:], lhsT=Dg[:, cc, k, :], rhs=glu[cc][:, 1 + s * 512 + k:1 + s * 512 + k + 512],
                    start=(k == 0), stop=(k == CONV_K - 1)),
                    reads=[(name, "Dg"), (name, "glu", cc)], writes=[res])
            P.op("act", lambda e, bank=bank, cc=cc: e.activation(out=co[cc][:], in_=bank[:], func=AF.Identity,
                                                               bias=dwb[:, cc:cc + 1]),
                 reads=[res] + PARS, writes=[(name, "co", s % 2, cc)])
            P.op("act", lambda e, bank=bank, cc=cc: e.activation(out=sq[cc][:], in_=bank[:], func=AF.Square,
                                                               bias=dwb[:, cc:cc + 1]),
                 reads=[res] + PARS, writes=[(name, "sq", cc)])

    def ln_pw(s):
        co = co2[s % 2]
        M, rm = ps[4], ("ps", 4)
        Q, rq = ps[5], ("ps", 5)
        for cc in range(2):
            P.op("pe", lambda e, cc=cc: e.matmul(M[:], lhsT=onesm[:], rhs=co[cc][:],
                                                 start=(cc == 0), stop=(cc == 1)),
                 reads=[(name, "onesm"), (name, "co", s % 2, cc)], writes=[rm])
        for cc in range(2):
            P.op("pe", lambda e, cc=cc: e.matmul(Q[:], lhsT=onesm[:], rhs=sq[cc][:],
                                                 start=(cc == 0), stop=(cc == 1)),
                 reads=[(name, "onesm"), (name, "sq", cc)], writes=[rq])
        P.op("act", lambda e: e.activation(out=m2[:], in_=M[:], func=AF.Square), reads=[rm], writes=[(name, "m2")])
        P.op("dve", lambda e: e.tensor_tensor(out=var[:], in0=Q[:], in1=m2[:], op=ALU.subtract),
             reads=[rq, (name, "m2")], writes=[(name, "var")])
        P.op("act", lambda e: e.activation(out=var[:], in_=var[:], func=AF.Ln, bias=C.ceps[:]),
             reads=[(name, "var"), "ceps"], writes=[(name, "var")])
        P.op("act", lambda e: e.activation(out=rstd[:], in_=var[:], func=AF.Exp, scale=-0.5), reads=[(name, "var")], writes=[(name, "rstd")])

    def ln_b(s):
        co = co2[s % 2]
        M, rm = ps[4], ("ps", 4)
        for cc in range(2):
            P.op("dve", lambda e, cc=cc: e.tensor_tensor(out=t1[cc][:], in0=co[cc][:], in1=M[:], op=ALU.subtract),
                 reads=[(name, "co", s % 2, cc), rm], writes=[(name, "t1", cc)])
            P.op("dve", lambda e, cc=cc: e.tensor_tensor(out=t1[cc][:], in0=t1[cc][:], in1=rstd[:], op=ALU.mult),
                 reads=[(name, "t1", cc), (name, "rstd")], writes=[(name, "t1", cc)])
            P.op("act", lambda e, cc=cc: e.activation(out=actb[cc][:], in_=t1[cc][:], func=AF.Silu,
                                                     bias=lnb[:, cc:cc + 1], scale=lng[:, cc:cc + 1]),
                 reads=[(name, "t1", cc)] + PARS, writes=[(name, "act", cc)])

    def pw_mm(s):
        for oc in range(2):
            bank, res = rbank()
            for cc in range(2):
                P.op("pe", lambda e, bank=bank, oc=oc, cc=cc: e.matmul(
                    bank[:], lhsT=pwb[:, cc, oc * 128:(oc + 1) * 128], rhs=actb[cc][:], start=(cc == 0), stop=(cc == 1)),
                    reads=[(name, "par2"), (name, "act", cc)], writes=[res])
            P.op("act", lambda e, bank=bank, oc=oc: e.copy(out=coT[:, oc, :], in_=bank[:]),
                 reads=[res], writes=[(name, "coT")])

    xr4 = [xr[0], xr[1], xt[2], xt[3]]
    xrr = [(name, "xr", 0), (name, "xr", 1), ("G", "xt", 2), ("G", "xt", 3)]

    def wout(s, blocks=(0, 1, 2, 3)):
        for b4 in blocks:
            b = s * 4 + b4
            i = b % 4
            P.dma("sp", xr4[i][:], x_in[b * 128:(b + 1) * 128, :], key=(name, "xr", i), writes=[xrr[i]])
            for hf in range(2):
                Y, ry = rbank()
                for kc in range(4):
                    if kc < 2:
                        lhs = coT[:, kc, b4 * 128:(b4 + 1) * 128]
                        rd = (name, "coT")
                    else:
                        lhs = poT[kc - 2][:, b * 128:(b + 1) * 128]
                        rd = (name, "poT", kc - 2)
                    P.op("pe", lambda e, Y=Y, lhs=lhs, kc=kc, hf=hf: e.matmul(
                        Y[:], lhsT=lhs, rhs=woa[:, kc, hf * 512:(hf + 1) * 512], start=(kc == 0), stop=(kc == 3)),
                        reads=[rd, (name, "woa")], writes=[ry])
                P.op("dve", lambda e, Y=Y, i=i, hf=hf: e.tensor_tensor(
                    out=xr4[i][:, hf * 512:(hf + 1) * 512], in0=xr4[i][:, hf * 512:(hf + 1) * 512], in1=Y[:], op=ALU.add),
                    reads=[ry, xrr[i]], writes=[xrr[i]])
            o = P.dma("sp", x_mid[b * 128:(b + 1) * 128, :], xr4[i][:], key=(name, "xo", i),
                      reads=[xrr[i]], writes=[("xd", xout_name, b)])
            P.out_ops.append(o)

    pool_dve(0)
    conv_mm(0)
    for s in range(nsub):
        if s >= 1:
            wout(s - 1, (0, 1))
        ln_pw(s)
        ln_b(s)
        if s + 1 < nsub:
            pool_dve(s + 1)
            conv_mm(s + 1)
        if s >= 1:
            wout(s - 1, (2, 3))
        pool_mm(s)
        pw_mm(s)
        if next_pre is not None and s in (1, 3):
            next_pre[(s + 1) // 2]()
            npdone = (s + 1) // 2
    if next_pre is not None:
        for k_ in range(npdone + 1, 3):
            next_pre[k_]()
    wout(nsub - 1)


GW = 64
NEG = -30000.0


def attn_plan(rows):
    nt = rows // 2
    r0 = lambda r: min(max(r - 4, 0), rows - 8)
    variants = []
    pairs = {}

    def key_of(qt, kt):
        valid = tuple(tuple(r0(2 * qt + i) <= 2 * kt + j < r0(2 * qt + i) + 8 for j in range(2)) for i in range(2))
        if not any(any(v) for v in valid):
            return None
        return (2 * kt - 2 * qt, valid)
    km = nt // 2
    order = [(qt, km) for qt in range(nt)] + [(qt, kt) for qt in range(nt) for kt in range(nt)]
    for qt, kt in order:
        key = key_of(qt, kt)
        if key is None:
            continue
        if key not in variants:
            variants.append(key)
        pairs[(qt, kt)] = variants.index(key)
    return nt, variants, pairs


def build_bias_table(rpb, rows):
    nt, variants, pairs = attn_plan(rows)
    H = rpb.shape[0]
    qc = np.arange(GW)
    c0 = np.clip(qc - 8, 0, GW - 16)
    kc = np.arange(GW)
    colvalid = (kc[:, None] >= c0[None, :]) & (kc[:, None] < c0[None, :] + 16)
    dc = np.clip(kc[:, None] - qc[None, :] + 15, 0, 30)
    tab = np.full((H, len(variants), 128, 128), NEG, np.float32)
    for v, (d, valid) in enumerate(variants):
        for i in range(2):
            for j in range(2):
                if not valid[i][j]:
                    continue
                dr = d + j - i + 7
                blk = rpb[:, dr][:, dc]
                tab[:, v, j * 64:(j + 1) * 64, i * 64:(i + 1) * 64] = np.where(colvalid[None], blk, NEG)
    return tab


def mixer_b(P, nc, C, T, x_in, x_mid, x_out, W, btab, ps, name="mb", xout_name="xb", next_pre=None):
    import contextlib
    rows = T // GW
    nt, variants, pairs = attn_plan(rows)
    NV = len(variants)
    nsub = T // 512
    with contextlib.ExitStack() as st:
        sb = lambda n, shape, dt: st.enter_context(nc.sbuf_tensor(name + n, shape, dt))
        qT = sb("qT", [128, 4, T], BF16)
        kT = sb("kT", [128, 4, T], BF16)
        vaug = sb("vaug", [128, nt, 8, 65], BF16)
        gq = sb("gq", [128, 1], F32)
        gk = sb("gk", [128, 1], F32)
        blk64 = sb("blk64", [128, 128], BF16)
        with contextlib.ExitStack() as st2:
            sb2 = lambda n, shape, dt: st2.enter_context(nc.sbuf_tensor(name + n, shape, dt))
            gb, xt, hb, st_ss, st_ms, st_rs = C.gb, C.xt, C.hb, C.ss, C.ms, C.rs
            hT = [C.hT0, sb2("hT1", [128, NKC, 512], BF16)]
            HRES = [("G", "hT0"), (name, "hT", 1)]
            winq = C.W0[:, 0:NKC * 512].rearrange("p (k n) -> p k n", k=NKC)
            winkv = sb2("winkv", [128, NKC, 1024], BF16)
            sqb = [sb2("sqb%d" % i, [128, 512], BF16) for i in range(2)]
            lnv = [sb2("lnv%d" % i, [128, 512], F32) for i in range(2)]
            rsd = [sb2("rsd%d" % i, [128, 512], F32) for i in range(2)]
            pT = [ps[0][:].bitcast(BF16), ps[1][:].bitcast(BF16)]
            bk = Banks(ps)

            winv = W["w_in"].rearrange("(kc p) n -> p kc n", p=128)
            for i in range(2):
                P.dma("pool", winkv[:, :, i * 512:(i + 1) * 512], winv[:, :, 1280 + i * 512:1280 + (i + 1) * 512],
                      key=(name, "winkv", i), writes=[(name, "winkv", i)])
            for tl, src in ((gq, "q_norm"), (gk, "k_norm")):
                a = W[src]
                for h2 in range(2):
                    P.dma("sp", tl[h2 * 64:(h2 + 1) * 64, :], bass.AP(a.tensor, a.offset, [[1, 64], [1, 1]]),
                          key=(name, "par", src), writes=[(name, "par", src, h2)])
            P.op("dve", lambda e: e.tensor_scalar(out=gq[:], in0=gq[:], scalar1=0.125, scalar2=None, op0=ALU.mult),
                 reads=[(name, "par", "q_norm", 0), (name, "par", "q_norm", 1)], writes=[(name, "gq8")])
            P.op("pool", lambda e: e.memset(blk64[:], 0.0), writes=[(name, "blk64")])
            for h2 in range(2):
                P.op("pool", lambda e, h2=h2: e.memset(blk64[h2 * 64:(h2 + 1) * 64, h2 * 64:(h2 + 1) * 64], 1.0 / 64.0),
                     writes=[(name, "blk64")])
            P.op("pool", lambda e: e.memset(vaug[:, :, :, 64:65], 1.0), writes=[(name, "vaug")])
            def nfront(b):
                i = b % 2
                norm_front(P, nc, "G", i, x_in[b * 128:(b + 1) * 128, :], xt, hb, st_ss, st_ms, st_rs, gb, C)

            def nback(b):
                i, b4, hs_ = b % 2, b % 4, (b // 4) % 2
                norm_back(P, nc, "G", i, hb, C, pT[i], ("ps", i))
                P.op("act", lambda e: e.copy(out=hT[hs_][:, :, b4 * 128:(b4 + 1) * 128],
                                             in_=pT[i].rearrange("p (k t) -> p k t", k=NKC)),
                     reads=[("ps", i)], writes=[HRES[hs_]])

            pend = None

            def finish(pd):
                (Qp, rq, par, qk, j, s_) = pd
                SS, rs = bk.next()
                P.op("pe", lambda e: e.matmul(SS[:], lhsT=blk64[:], rhs=sqb[par][:], start=True, stop=True),
                     reads=[(name, "blk64"), (name, "sqb", par)], writes=[rs])
                P.op("act", lambda e: e.activation(out=lnv[par][:], in_=SS[:], func=AF.Ln, bias=C.ceps[:]),
                     reads=[rs, "ceps"], writes=[(name, "lnv", par)])
                P.op("act", lambda e: e.activation(out=rsd[par][:], in_=lnv[par][:], func=AF.Exp, scale=-0.5),
                     reads=[(name, "lnv", par)], writes=[(name, "rsd", par)])
                cols = slice(s_ * 512, (s_ + 1) * 512)
                if qk == 1:
                    P.op("dve", lambda e: e.scalar_tensor_tensor(
                        out=kT[:, j, cols], in0=Qp[:], scalar=gk[:, 0:1], in1=rsd[par][:], op0=ALU.mult, op1=ALU.mult),
                        reads=[rq, (name, "rsd", par), (name, "par", "k_norm", 0), (name, "par", "k_norm", 1)], writes=[(name, "kT", j)])
                else:
                    P.op("dve", lambda e: e.scalar_tensor_tensor(
                        out=qT[:, j, cols], in0=Qp[:], scalar=gq[:, 0:1], in1=rsd[par][:], op0=ALU.mult, op1=ALU.mult),
                        reads=[rq, (name, "rsd", par), (name, "gq8")], writes=[(name, "qT", j)])

            SCHED = {0: [("f", 0), ("f", 1)], 3: [("b", 0)], 4: [("f", 2)], 6: [("b", 1)], 7: [("f", 3)],
                     9: [("b", 2)], 11: [("b", 3)]}

            def hook(s_, t_):
                if s_ + 1 >= nsub:
                    return
                for kind, bb in SCHED.get(t_, ()):
                    (nfront if kind == "f" else nback)((s_ + 1) * 4 + bb)

            cidx = 0
            for s in range(nsub):
                hs = s % 2
                ntask = 0
                for qk in (0, 1):
                    for j in range(4):
                        par = cidx % 2
                        cidx += 1
                        wsrc = winq if qk == 0 else winkv
                        wres = ("G", "W0") if qk == 0 else (name, "winkv", 0)
                        col0 = j * 128
                        Qp, rq = bk.next()
                        for kc in range(NKC):
                            if kc == 2 and pend is not None:
                                finish(pend)
                                pend = None
                            P.op("pe", lambda e, kc=kc, Qp=Qp, col0=col0, hs=hs, wsrc=wsrc: e.matmul(
                                Qp[:], lhsT=wsrc[:, kc, col0:col0 + 128], rhs=hT[hs][:, kc, :],
                                start=(kc == 0), stop=(kc == NKC - 1)),
                                reads=[wres, HRES[hs]], writes=[rq])
                        P.op("act", lambda e, Qp=Qp, par=par: e.activation(out=sqb[par][:], in_=Qp[:], func=AF.Square),
                             reads=[rq], writes=[(name, "sqb", par)])
                        pend = (Qp, rq, par, qk, j, s)
                        hook(s, ntask)
                        ntask += 1
                for b4 in range(4):
                    b = s * 4 + b4
                    Vp, rv = bk.next()
                    for kc in range(NKC):
                        if kc == 2 and pend is not None:
                            finish(pend)
                            pend = None
                        P.op("pe", lambda e, kc=kc, Vp=Vp, b4=b4, hs=hs: e.matmul(
                            Vp[:], lhsT=hT[hs][:, kc, b4 * 128:(b4 + 1) * 128], rhs=winkv[:, kc, 512:1024],
                            start=(kc == 0), stop=(kc == NKC - 1)),
                            reads=[(name, "winkv", 1), HRES[hs]], writes=[rv])
                    P.op("act", lambda e, Vp=Vp, b=b: e.copy(out=vaug[:, b, :, 0:64],
                                                            in_=Vp[:].rearrange("p (h d) -> p h d", h=8)),
                         reads=[rv], writes=[(name, "vaug")])
                    hook(s, 8 + b4)
        P.barrier()
        with contextlib.ExitStack() as st3:
            sb3 = lambda n, shape, dt: st3.enter_context(nc.sbuf_tensor(name + n, shape, dt))
            atok = sb3("atok", [128, nt, 512], BF16)
            bt = [sb3("bt%d" % i, [128, NV, 128], BF16) for i in range(2)]
            NPT = 8
            PT = [sb3("PT%d" % i, [128, 768], BF16) for i in range(NPT)]
            wob = sb3("wob", [128, 4, D], BF16)
            NXR = 3
            xr = [sb3("xr%d" % i, [128, D], F32) for i in range(NXR)]
            rden = [sb3("rden%d" % i, [128, 1], F32) for i in range(4)]
            aTt = [sb3("aTt%d" % i, [128, 4, 128], BF16) for i in range(2)]
            NKZ = 4
            kz = [[sb3("kz%d_%d" % (h2, i), [128, 128], BF16) for i in range(NKZ)] for h2 in range(2)]
            for h2 in range(2):
                for i in range(NKZ):
                    P.op("pool", lambda e, h2=h2, i=i: e.memset(kz[h2][i][:], 0.0), writes=[(name, "kz", h2, i)])

            wov = W["w_out"].rearrange("(kc p) n -> p kc n", p=128)
            P.dma("pool", wob[:], wov[:, 4:8, :], key=(name, "wob"), writes=[(name, "wob")])
            btv = btab.rearrange("h v k q -> h k v q")
            qts_of = {kt: sorted(q for (q, k) in pairs if k == kt) for kt in range(nt)}
            kts_of = {qt: sorted(k for (q, k) in pairs if q == qt) for qt in range(nt)}
            sbanks = [(2, 3), (4, 5)]
            oslot = [0]
            sidx = 0

            def pv(h, qt):
                kts = kts_of[qt]
                ob = (0, 1, 6, 7)[oslot[0] % 4]
                rdi = oslot[0] % 4
                oslot[0] += 1
                O = ps[ob][:, 0:65]
                ro = ("ps", ob)
                for n_, k2 in enumerate(kts):
                    off = (qt - qts_of[k2][0]) * 128
                    P.op("pe", lambda e, k2=k2, off=off, n_=n_: e.matmul(
                        O, lhsT=PT[k2 % NPT][:, off:off + 128], rhs=vaug[:, k2, h, :],
                        start=(n_ == 0), stop=(n_ == len(kts) - 1)),
                        reads=[(name, "PT", k2 % NPT), (name, "vaug")], writes=[ro])
                rd = rden[rdi]
                P.op("dve", lambda e: e.reciprocal(out=rd[:], in_=O[:, 64:65]),
                     reads=[ro], writes=[(name, "rden", rdi)])
                P.op("dve", lambda e: e.tensor_scalar(
                    out=atok[:, qt, h * 64:(h + 1) * 64], in0=O[:, 0:64], scalar1=rd[:, 0:1], scalar2=None, op0=ALU.mult),
                    reads=[ro, (name, "rden", rdi)], writes=[(name, "atok", qt)])

            for h in range(8):
                j, h2 = h // 2, h % 2
                bsl = h % 2
                P.dma("pool", bt[bsl][:], btv[h], key=(name, "bt", bsl), writes=[(name, "bt", bsl)])
                todo = []
                for kt in range(nt):
                    qts = qts_of[kt]
                    qa, qb = qts[0], qts[-1]
                    assert qts == list(range(qa, qb + 1))
                    N = (qb - qa + 1) * 128
                    bA, bB = sbanks[sidx % 2]
                    sidx += 1
                    pslot = kt % NPT
                    segs = [(0, min(N, 512), bA)] + ([(512, N, bB)] if N > 512 else [])
                    kzi = kt % NKZ
                    kzt = kz[h2][kzi]
                    hr = slice(h2 * 64, (h2 + 1) * 64)
                    P.op("pool", lambda e, kzt=kzt, hr=hr, kt=kt, j=j: e.tensor_copy(
                        out=kzt[hr, :], in_=kT[hr, j, kt * 128:(kt + 1) * 128]),
                        reads=[(name, "kT", j)], writes=[(name, "kz", h2, kzi)])
                    for (c0, c1, bnk) in segs:
                        P.op("pe", lambda e, c0=c0, c1=c1, bnk=bnk, kzt=kzt, qa=qa, j=j: e.matmul(
                            ps[bnk][:, 0:c1 - c0], lhsT=kzt[:],
                            rhs=qT[:, j, qa * 128 + c0:qa * 128 + c1], start=True, stop=False),
                            reads=[(name, "qT", j), (name, "kz", h2, kzi)], writes=[("ps", bnk)])
                        tis = list(range(c0 // 128, c1 // 128))
                        runs = []
                        for ti in tis:
                            v = pairs[(qa + ti, kt)]
                            if runs and runs[-1][1] + runs[-1][2] == v and runs[-1][0] + runs[-1][2] == ti:
                                runs[-1][2] += 1
                            else:
                                runs.append([ti, v, 1])
                        for ri, (ti, v, n) in enumerate(runs):
                            P.op("pe", lambda e, ti=ti, v=v, n=n, c0=c0, bnk=bnk, bsl=bsl, last=(ri == len(runs) - 1): e.matmul(
                                ps[bnk][:, ti * 128 - c0:(ti + n) * 128 - c0], lhsT=C.ident[:],
                                rhs=bt[bsl][:, v:v + n, :], start=False, stop=last),
                                reads=["ident", (name, "bt", bsl)], writes=[("ps", bnk)])
                        P.op("act", lambda e, c0=c0, c1=c1, bnk=bnk, pslot=pslot: e.activation(
                            out=PT[pslot][:, c0:c1], in_=ps[bnk][:, 0:c1 - c0], func=AF.Exp),
                            reads=[("ps", bnk)], writes=[(name, "PT", pslot)])
                    for qt in todo:
                        pv(h, qt)
                    todo = [qt for qt in range(nt) if kts_of[qt][-1] == kt]
                for qt in todo:
                    pv(h, qt)
            pT = [ps[0][:].bitcast(BF16), ps[1][:].bitcast(BF16)]
            ybank = [0]

            def front(b):
                i = b % 2
                P.dma("sp", xr[b % NXR][:], x_mid[b * 128:(b + 1) * 128, :], key=(name, "xr", b % NXR),
                      writes=[(name, "xr", b % NXR)])
                for c in range(4):
                    P.op("pe", lambda e, c=c: e.transpose(out=pT[i][:, c * 128:(c + 1) * 128],
                                                          in_=atok[:, b, c * 128:(c + 1) * 128], identity=C.ident[:]),
                         reads=[(name, "atok", b), "ident"], writes=[("ps", i)])
                P.op("act", lambda e: e.copy(out=aTt[i][:], in_=pT[i][:, 0:512].rearrange("p (k t) -> p k t", k=4)),
                     reads=[("ps", i)], writes=[(name, "aTt", i)])

            def back(b):
                i = b % 2
                xi = b % NXR
                for hf in range(2):
                    yb = 2 + (ybank[0] % 6)
                    ybank[0] += 1
                    for kc in range(4):
                        P.op("pe", lambda e, yb=yb, kc=kc, hf=hf: e.matmul(
                            ps[yb][:], lhsT=aTt[i][:, kc, :], rhs=wob[:, kc, hf * 512:(hf + 1) * 512],
                            start=(kc == 0), stop=(kc == 3)),
                            reads=[(name, "aTt", i), (name, "wob")], writes=[("ps", yb)])
                    P.op("dve", lambda e, yb=yb, hf=hf: e.tensor_tensor(
                        out=xr[xi][:, hf * 512:(hf + 1) * 512], in0=xr[xi][:, hf * 512:(hf + 1) * 512], in1=ps[yb][:], op=ALU.add),
                        reads=[("ps", yb), (name, "xr", xi)], writes=[(name, "xr", xi)])
                o = P.dma("sp", x_out[b * 128:(b + 1) * 128, :], xr[xi][:], key=(name, "xo", xi),
                          reads=[(name, "xr", xi)], writes=[("xd", xout_name, b)])
                P.out_ops.append(o)

            front(0)
            for b in range(nt):
                if b + 1 < nt:
                    front(b + 1)
                back(b)
                if next_pre is not None:
                    for k_, bb in enumerate((3, min(8, nt - 2), min(13, nt - 1))):
                        if b == bb:
                            next_pre[k_]()


T_SEQ = 4096
DEPTH = 2
PSHAPES = dict(
    ffn1_norm=[D], ffn1_gate=[D, DFF], ffn1_up=[D, DFF], ffn1_down=[DFF, D], mix_norm=[D], w_in=[D, 2304],
    conv_dw=[31, 256], conv_dw_b=[256], conv_ln_g=[256], conv_ln_b=[256], conv_pw=[256, 256],
    pool_w=[4, 64, 64], pool_scale=[256], q_norm=[64], k_norm=[64], w_out=[1024, 1024],
    ffn2_norm=[D], ffn2_gate=[D, DFF], ffn2_up=[D, DFF], ffn2_down=[DFF, D])


def build_program(L, T=T_SEQ, TT=1024, GRP=2):
    import contextlib
    rows = T // GW
    NV = len(attn_plan(rows)[1])
    nc = bass.Bass("TRN2", target_bir_lowering=False)
    x = nc.dram_tensor("x", [T, D], F32, kind="ExternalInput").ap()
    Wf = {k: nc.dram_tensor(k, [L] + v, F32, kind="ExternalInput").ap() for k, v in PSHAPES.items()}
    btab = nc.dram_tensor("btab", [L, 8, NV, 128, 128], F32, kind="ExternalInput").ap()
    y = nc.dram_tensor("y", [T, D], F32, kind="ExternalOutput").ap()
    scr = [nc.dram_tensor("scr%d" % i, [T, D], F32).ap() for i in range(4)]
    with contextlib.ExitStack() as st:
        ps = [st.enter_context(nc.psum_tensor("ps%d" % i, [128, 512], F32)) for i in range(8)]
        C = Consts(nc, st)
        P = Prog(nc)
        C.init(P)
        def W0v(n):
            return C.W0[:, 0:NKC * n].rearrange("p (k n) -> p k n", k=NKC)

        def pre_ffn(x_ap, xname, g_ap, wg, wu):
            v = C.W0[:, 0:NKC * 2 * 256].rearrange("p (k w c) -> p k w c", k=NKC, w=2)
            wgv = wg.rearrange("(kc p) n -> p kc n", p=128)
            wuv = wu.rearrange("(kc p) n -> p kc n", p=128)
            loads = [(v[:, :, 0, :], wgv[:, :, 0:256]), (v[:, :, 1, :], wuv[:, :, 0:256])]
            return preamble(P, nc, C, ps, x_ap, xname, g_ap, loads)

        def pre_mix(x_ap, xname, W, c0, n):
            winv = W["w_in"].rearrange("(kc p) n -> p kc n", p=128)
            loads = [(W0v(n), winv[:, :, c0:c0 + n])]
            return preamble(P, nc, C, ps, x_ap, xname, W["mix_norm"], loads)

        phases = []
        cur, curname = x, "xin"
        for l in range(L):
            W = {k: v[l] for k, v in Wf.items()}
            xo, xoname = (y, "y") if l == L - 1 else (scr[3], "xo%d" % l)
            phases.append(dict(kind="f1", l=l, W=W, xin=cur, xin_name=curname, xout=scr[0], xout_name="xa%d" % l))
            phases.append(dict(kind="ma", l=l, W=W, xin=scr[0], xin_name="xa%d" % l, xout=scr[1], xout_name="xm%d" % l))
            phases.append(dict(kind="mb", l=l, W=W, xin=scr[0], xin_name="xa%d" % l, xmid=scr[1], xout=scr[2], xout_name="xb%d" % l))
            phases.append(dict(kind="f2", l=l, W=W, xin=scr[2], xin_name="xb%d" % l, xout=xo, xout_name=xoname))
            cur, curname = xo, xoname

        def mk_pre(ph):
            W = ph["W"]
            if ph["kind"] == "f1":
                return pre_ffn(ph["xin"], ph["xin_name"], W["ffn1_norm"], W["ffn1_gate"], W["ffn1_up"])
            if ph["kind"] == "f2":
                return pre_ffn(ph["xin"], ph["xin_name"], W["ffn2_norm"], W["ffn2_gate"], W["ffn2_up"])
            if ph["kind"] == "ma":
                return pre_mix(ph["xin"], ph["xin_name"], W, 0, 512)
            return pre_mix(ph["xin"], ph["xin_name"], W, 768, 512)

        for st_ in mk_pre(phases[0]):
            st_()
        for i, ph in enumerate(phases):
            W, l = ph["W"], ph["l"]
            nxt = mk_pre(phases[i + 1]) if i + 1 < len(phases) else None
            if ph["kind"] in ("f1", "f2"):
                k = "ffn1" if ph["kind"] == "f1" else "ffn2"
                with contextlib.ExitStack() as s1:
                    ffn_phase(P, nc, s1, C, T, ph["xin"], ph["xout"], W[k + "_norm"], W[k + "_gate"], W[k + "_up"],
                              W[k + "_down"], ps, TT=TT, GRP=GRP, name="%s_%d" % (ph["kind"], l),
                              xout_name=ph["xout_name"], next_pre=nxt)
            elif ph["kind"] == "ma":
                with contextlib.ExitStack() as s2:
                    mixer_a(P, nc, s2, C, T, ph["xin"], ph["xout"], W, ps, name="ma%d" % l,
                            xout_name=ph["xout_name"], next_pre=nxt)
            else:
                mixer_b(P, nc, C, T, ph["xin"], ph["xmid"], ph["xout"], W, btab[l], ps, name="mb%d" % l,
                        xout_name=ph["xout_name"], next_pre=nxt)
            P.barrier()
        nblk = T // 128
        P.emit(final_wait_ops=P.out_ops[-nblk:])
    return nc


_PROGS = {}


def _get_prog(L):
    if L not in _PROGS:
        _PROGS[L] = build_program(L)
    return _PROGS[L]


FUSED = True


def kernel(**inputs):
    x = np.ascontiguousarray(np.asarray(inputs["x"], dtype=np.float32))
    B = x.shape[0]
    assert B == 8 and x.shape[1] == T_SEQ and x.shape[2] == D
    Wn = {k: np.ascontiguousarray(np.asarray(inputs[k], dtype=np.float32)) for k in PSHAPES}
    rpb = np.asarray(inputs["rpb"], dtype=np.float32)
    btab = np.stack([build_bias_table(rpb[l], T_SEQ // GW) for l in range(DEPTH)], 0)
    cores = list(range(B))
    if FUSED:
        nc = _get_prog(DEPTH)
        in_maps = [dict(x=x[b], btab=btab, **Wn) for b in cores]
        res = run_bass_kernel_spmd(nc, in_maps, core_ids=cores)
        return np.stack([np.asarray(res.results[b]["y"]) for b in cores], 0).astype(np.float32)
    cur = x
    nc = _get_prog(1)
    for l in range(DEPTH):
        in_maps = [dict(x=cur[b], btab=btab[l:l + 1], **{k: v[l:l + 1] for k, v in Wn.items()}) for b in cores]
        res = run_bass_kernel_spmd(nc, in_maps, core_ids=cores)
        cur = np.stack([np.asarray(res.results[b]["y"]) for b in cores], 0).astype(np.float32)
    return cur
```

```python
import numpy as np
import concourse.bass as bass
import concourse.mybir as mybir
from concourse.bass_utils import run_bass_kernel_spmd


F32 = mybir.dt.float32
BF16 = mybir.dt.bfloat16
AF = mybir.ActivationFunctionType
ALU = mybir.AluOpType

ENGS = ("pe", "act", "dve", "pool", "sp")


class Op:
    __slots__ = ("eng", "fn", "deps", "sig", "sigval", "is_dma", "dsem", "dval", "idx")

    def __init__(self, eng, fn, is_dma=False):
        self.eng = eng
        self.fn = fn
        self.deps = []
        self.sig = False
        self.sigval = 0
        self.is_dma = is_dma
        self.dsem = None
        self.dval = 0


class Prog:
    def __init__(self, nc):
        self.nc = nc
        self.ops = []
        self.last_w = {}
        self.readers = {}
        self.dma_cnt = {}
        self.dma_keys = []
        self.out_ops = []
        self.phase_keys = {}

    def _add(self, op, reads, writes):
        deps = set()
        for r in reads:
            w = self.last_w.get(r)
            if w is not None:
                deps.add(w)
        for r in writes:
            w = self.last_w.get(r)
            if w is not None:
                deps.add(w)
            lastc = {}
            for rd in self.readers.get(r, ()):
                if rd.is_dma:
                    deps.add(rd)
                else:
                    lastc[rd.eng] = rd
            deps.update(lastc.values())
        deps.discard(op)
        raw = set(self.last_w.get(r) for r in reads)
        for d in deps:
            if (d not in raw) and (not d.is_dma) and (not op.is_dma) and d.eng == op.eng and d.eng == "pe":
                continue
            op.deps.append(d)
        for r in reads:
            self.readers.setdefault(r, []).append(op)
        for r in writes:
            self.last_w[r] = op
            self.readers[r] = []
        op.idx = len(self.ops)
        self.ops.append(op)
        return op

    def op(self, eng, fn, reads=(), writes=()):
        return self._add(Op(eng, fn), reads, writes)

    def dma(self, eng, out, in_, key, reads=(), writes=(), **kw):
        def fn(e, out=out, in_=in_, kw=kw):
            return e.dma_start(out=out, in_=in_, **kw)
        o = Op(eng, fn, is_dma=True)
        cls = "sw" if eng == "pool" else "hw"
        if key not in self.phase_keys:
            self.phase_keys[key] = (cls, sum(1 for v in self.phase_keys.values() if v[0] == cls))
        slot = self.phase_keys[key]
        assert slot[0] == cls, "a DMA key must stay on one queue class"
        if slot not in self.dma_cnt:
            self.dma_cnt[slot] = 0
            self.dma_keys.append(slot)
        self.dma_cnt[slot] += 16
        o.dsem = slot
        o.dval = self.dma_cnt[slot]
        return self._add(o, reads, writes)

    def emit(self, final_wait_ops=()):
        nc = self.nc
        raw_same = set()
        for o in self.ops:
            for d in o.deps:
                if not d.is_dma:
                    d.sig = True
        cnt = {e: 0 for e in ENGS}
        for o in self.ops:
            if not o.is_dma and o.sig:
                cnt[o.eng] += 1
                o.sigval = cnt[o.eng]
        self.maxsig = dict(cnt)
        per_eng = {e: [o for o in self.ops if o.eng == e] for e in ENGS}
        import contextlib
        with contextlib.ExitStack() as st:
            esem = {e: st.enter_context(nc.semaphore("s_" + e)) for e in ENGS}
            dsem = {k: st.enter_context(nc.semaphore("d_%s%d" % k)) for k in self.dma_keys}
            block = st.enter_context(nc.Block())

            def body(ename, e):
                known = {}
                for o in per_eng[ename]:
                    need = {}
                    for d in o.deps:
                        if d.is_dma:
                            k = ("d", d.dsem)
                            v = d.dval
                        else:
                            if d.eng == ename:
                                pass
                            k = ("e", d.eng)
                            v = d.sigval
                        if known.get(k, 0) >= v:
                            continue
                        if need.get(k, 0) < v:
                            need[k] = v
                    for k, v in need.items():
                        sem = dsem[k[1]] if k[0] == "d" else esem[k[1]]
                        e.wait_ge(sem, v)
                        known[k] = v
                    if o.fn is None:
                        continue
                    ins = o.fn(e)
                    if o.is_dma:
                        ins.then_inc(dsem[o.dsem], 16)
                    elif o.sig:
                        ins.then_inc(esem[ename], 1)
                if ename == "sp":
                    for o in final_wait_ops:
                        e.wait_ge(dsem[o.dsem], o.dval)

            @block.tensor
            def _(e):
                body("pe", e)

            @block.scalar
            def _(e):
                body("act", e)

            @block.vector
            def _(e):
                body("dve", e)

            @block.gpsimd
            def _(e):
                body("pool", e)

            @block.sync
            def _(e):
                body("sp", e)


def _barrier(self):
    last = {}
    for o in self.ops:
        if o.fn is None:
            continue
        if o.is_dma:
            last[("d", o.dsem)] = o
        else:
            last[("e", o.eng)] = o
    deps = list(last.values())
    for e in ENGS:
        b = Op(e, None)
        b.deps = list(deps)
        b.idx = len(self.ops)
        self.ops.append(b)
    self.last_w = {}
    self.readers = {}
    self.phase_keys = {}


Prog.barrier = _barrier


D = 1024
DFF = 2816
NKC = D // 128
NFC = DFF // 128
EPS = 1e-6


def bcast_rows(ap1d, n, parts=128):
    return bass.AP(ap1d.tensor, ap1d.offset, [[0, parts], [1, n]])


class Consts:
    def __init__(self, nc, st):
        self.ident = st.enter_context(nc.sbuf_tensor("ident", [128, 128], BF16))
        self.cneg = st.enter_context(nc.sbuf_tensor("cneg", [128, 1], F32))
        self.ceps = st.enter_context(nc.sbuf_tensor("ceps", [128, 1], F32))
        sb = lambda n, shape, dt: st.enter_context(nc.sbuf_tensor("G" + n, shape, dt))
        self.gb = sb("gb", [128, D], F32)
        self.xt = [sb("xt%d" % i, [128, D], F32) for i in range(2)]
        self.hb = [sb("hb%d" % i, [128, D], BF16) for i in range(2)]
        self.ss = [sb("ss%d" % i, [128, 1], F32) for i in range(2)]
        self.ms = [sb("ms%d" % i, [128, 1], F32) for i in range(2)]
        self.rs = [sb("rs%d" % i, [128, 1], F32) for i in range(2)]
        self.hT0 = sb("hT0", [128, NKC, 512], BF16)
        self.W0 = sb("W0", [128, 4096], BF16)

    def init(self, P):
        ident, cneg, ceps = self.ident, self.cneg, self.ceps
        P.op("pool", lambda e: e.memset(ceps[:], EPS), writes=["ceps"])
        P.op("pool", lambda e: e.memset(ident[:], 0.0), writes=["ident"])
        P.op("pool", lambda e: e.affine_select(out=ident[:], in_=ident[:], pattern=[[-1, 128]],
                                                compare_op=ALU.not_equal, fill=1.0, base=0,
                                                channel_multiplier=1), reads=["ident"], writes=["ident"])
        P.op("pool", lambda e: e.memset(cneg[:], -0.5), writes=["cneg"])


def rms_rstd(P, tag, ss, ms, rstd, cneg, n):
    P.op("pool", lambda e: e.tensor_scalar(out=ms, in0=ss, scalar1=1.0 / n, scalar2=EPS,
                                           op0=ALU.mult, op1=ALU.add),
         reads=[("ss", tag)], writes=[("ms", tag)])
    P.op("pool", lambda e: e.tensor_tensor(out=rstd, in0=ms, in1=cneg, op=ALU.pow),
         reads=[("ms", tag), "cneg"], writes=[("rstd", tag)])


def ffn_phase(P, nc, st, C, T, x_in, x_out, g_ap, wg, wu, wd, ps, TT=1024, GRP=2, name="f", xout_name="x", next_pre=None):
    nblk = TT // 128
    nsub = TT // 512
    npass = T // TT
    ngrp = NFC // GRP
    sb = lambda n, shape, dt: st.enter_context(nc.sbuf_tensor(name + n, shape, dt))
    assert TT == 1024 and GRP == 2
    gb, xt, hb, st_ss, st_ms, st_rs = C.gb, C.xt, C.hb, C.ss, C.ms, C.rs
    xr = [sb("xr%d" % i, [128, D], F32) for i in range(2)]
    hTs = [C.hT0] + [sb("hT%d" % i, [128, NKC, 512], BF16) for i in range(1, nsub)]
    wgu = [C.W0[:, 0:NKC * 2 * GRP * 128].rearrange("p (k w c) -> p k w c", k=NKC, w=2),
           sb("wgu1", [128, NKC, 2, GRP * 128], BF16), sb("wgu2", [128, NKC, 2, GRP * 128], BF16)]
    WRES = [("G", "W0"), (name, "wgu", 1), (name, "wgu", 2)]
    aT = sb("aT", [128, NFC, TT], BF16)
    wdb = sb("wdb", [128, NFC, D], BF16)
    sg = [sb("sg%d" % i, [128, 512], F32) for i in range(2)]
    ident = C.ident

    pT = [ps[0][:].bitcast(BF16), ps[1][:].bitcast(BF16)]
    wgv = wg.rearrange("(kc p) n -> p kc n", p=128)
    wuv = wu.rearrange("(kc p) n -> p kc n", p=128)
    wdv = wd.rearrange("(m p) n -> p m n", p=128)

    def load_wdb(i):
        P.dma("pool", wdb[:, 2 * i:2 * i + 2, :], wdv[:, 2 * i:2 * i + 2, :],
              key=(name, "wdb", i), writes=[(name, "wdb", i)])

    def load_wgu(p, g):
        slot = (p * ngrp + g) % 3
        c0 = g * GRP * 128
        P.dma("pool", wgu[slot][:, :, 0, :], wgv[:, :, c0:c0 + GRP * 128], key=WRES[slot],
              writes=[WRES[slot]])
        P.dma("pool", wgu[slot][:, :, 1, :], wuv[:, :, c0:c0 + GRP * 128], key=WRES[slot],
              writes=[WRES[slot]])

    ybank = [0]
    HRES = [("G", "hT0")] + [(name, "hT", i) for i in range(1, nsub)]

    def a1(p, b):
        t0 = p * TT
        i = b % 2
        P.dma("sp", xt[i][:], x_in[t0 + b * 128:t0 + (b + 1) * 128, :], key=("G", "xt", i),
              writes=[("G", "xt", i)])
        P.op("dve", lambda e, i=i: e.scalar_tensor_tensor(out=hb[i][:], in0=xt[i][:], scalar=1.0, in1=xt[i][:],
                                                         op0=ALU.mult, op1=ALU.mult, accum_out=st_ss[i][:]),
             reads=[("G", "xt", i)], writes=[("G", "hb", i), ("ss", ("G", i))])
        rms_rstd(P, ("G", i), st_ss[i][:], st_ms[i][:], st_rs[i][:], C.cneg[:], D)

    def a2(p, b):
        i = b % 2
        P.op("dve", lambda e, i=i: e.scalar_tensor_tensor(out=hb[i][:], in0=xt[i][:], scalar=st_rs[i][:], in1=gb[:],
                                                         op0=ALU.mult, op1=ALU.mult),
             reads=[("G", "xt", i), ("rstd", ("G", i)), ("G", "gb")], writes=[("G", "hb", i)])

    def a_back(p, b):
        i = b % 2
        for kc in range(NKC):
            P.op("pe", lambda e, i=i, kc=kc: e.transpose(out=pT[i][:, kc * 128:(kc + 1) * 128],
                                                       in_=hb[i][:, kc * 128:(kc + 1) * 128], identity=ident[:]),
                 reads=[("G", "hb", i), "ident"], writes=[("ps", i)])
        P.op("act", lambda e, i=i, b=b: e.copy(out=hTs[b // 4][:, :, (b % 4) * 128:(b % 4 + 1) * 128],
                                              in_=pT[i].rearrange("p (k t) -> p k t", k=NKC)),
             reads=[("ps", i)], writes=[HRES[b // 4]])

    def c_mm(p, b):
        t0 = p * TT
        i = b % 2
        P.dma("sp", xr[i][:], x_in[t0 + b * 128:t0 + (b + 1) * 128, :], key=(name, "xr", i),
              writes=[(name, "xr", i)])
        ybs = []
        for hf in range(2):
            yb = (6, 7, 2, 3, 4, 5)[(2 * b + hf - (2 * nblk - 2)) % 6]
            ybs.append(yb)
            Y = ps[yb]
            for m in range(NFC):
                P.op("pe", lambda e, m=m, b=b, hf=hf, Y=Y: e.matmul(
                    Y[:], lhsT=aT[:, m, b * 128:(b + 1) * 128], rhs=wdb[:, m, hf * 512:(hf + 1) * 512],
                    start=(m == 0), stop=(m == NFC - 1)),
                    reads=[(name, "aT", b // 4), (name, "wdb", m // 2)], writes=[("ps", yb)])
        return ybs

    def c_evac(p, b, ybs):
        t0 = p * TT
        i = b % 2
        for hf in range(2):
            yb = ybs[hf]
            Y = ps[yb]
            P.op("dve", lambda e, Y=Y, i=i, hf=hf: e.scalar_tensor_tensor(
                out=xr[i][:, hf * 512:(hf + 1) * 512], in0=Y[:], scalar=0.5, in1=xr[i][:, hf * 512:(hf + 1) * 512],
                op0=ALU.mult, op1=ALU.add),
                reads=[("ps", yb), (name, "xr", i)], writes=[(name, "xr", i)])
        o = P.dma("sp", x_out[t0 + b * 128:t0 + (b + 1) * 128, :], xr[i][:], key=(name, "xo", i),
                  reads=[(name, "xr", i)], writes=[("xd", xout_name, (t0 // 128) + b)])
        P.out_ops.append(o)

    pro = [[lambda: a_back(0, 4), lambda: a_back(0, 5), lambda: (a1(0, 6), a2(0, 6)), lambda: (a1(0, 7), a2(0, 7))],
           [lambda: a_back(0, 6), lambda: a_back(0, 7)]]
    wloaded = {0}

    def gate_up(p, g, s):
        q = p * ngrp + g
        slot = q % 3
        for c in range(GRP):
            m = g * GRP + c
            par = (s * GRP + c) % 2
            G = ps[2 + 2 * par]
            U = ps[3 + 2 * par]
            for which, dst in ((0, G), (1, U)):
                for kc in range(NKC):
                    P.op("pe", lambda e, kc=kc, which=which, dst=dst, slot=slot, c=c, s=s: e.matmul(
                        dst[:], lhsT=wgu[slot][:, kc, which, c * 128:(c + 1) * 128],
                        rhs=hTs[s][:, kc, :], start=(kc == 0), stop=(kc == NKC - 1)),
                        reads=[WRES[slot], HRES[s]], writes=[("ps", 2 + 2 * par + which)])
            P.op("act", lambda e, G=G, par=par: e.activation(out=sg[par][:], in_=G[:], func=AF.Silu),
                 reads=[("ps", 2 + 2 * par)], writes=[(name, "sg", par)])
            P.op("dve", lambda e, U=U, par=par, m=m, s=s: e.tensor_tensor(
                out=aT[:, m, s * 512:(s + 1) * 512], in0=sg[par][:], in1=U[:], op=ALU.mult),
                reads=[(name, "sg", par), ("ps", 3 + 2 * par)], writes=[(name, "aT", s)])

    def prefetch(q):
        for q2 in (q + 1, q + 2):
            if q2 < npass * ngrp and q2 not in wloaded:
                wloaded.add(q2)
                load_wgu(q2 // ngrp, q2 % ngrp)

    for p in range(npass):
        if p == 0:
            prefetch(0)
            for b in (4, 5):
                a1(0, b)
                a2(0, b)
            for i in range(0, 2):
                load_wdb(i)
            gate_up(0, 0, 0)
            for fn_ in pro.pop(0):
                fn_()
            gate_up(0, 1, 0)
            for fn_ in pro.pop(0):
                fn_()
            gate_up(0, 0, 1)
            gate_up(0, 1, 1)
            g_start = 2
        else:
            g_start = 0
        for g in range(g_start, ngrp):
            prefetch(p * ngrp + g)
            if p == 0:
                for i in range(g * 11 // ngrp, (g + 1) * 11 // ngrp):
                    load_wdb(i)
            for s in range(nsub):
                gate_up(p, g, s)
        nxt = p + 1 < npass
        if nxt:
            a1(p + 1, 0)
            a2(p + 1, 0)
        elif next_pre is not None and npass > 1:
            next_pre[0]()
        for b in range(nblk):
            ybs = c_mm(p, b)
            if nxt:
                a_back(p + 1, b)
                if b + 1 < nblk:
                    a1(p + 1, b + 1)
            c_evac(p, b, ybs)
            if nxt and b + 1 < nblk:
                a2(p + 1, b + 1)
            if next_pre is not None and not nxt:
                if npass == 1:
                    for k_, bb in enumerate((3, 5, 7)):
                        if b == bb:
                            next_pre[k_]()
                else:
                    for k_, bb in ((1, 3), (2, 6)):
                        if b == bb:
                            next_pre[k_]()


CONV_K = 31
POOL_WINDOWS = (2, 4, 8, 16)


class Banks:
    def __init__(self, ps, lo=2, hi=8):
        self.ps, self.lo, self.n, self.i = ps, lo, hi - lo, 0

    def next(self):
        k = self.lo + (self.i % self.n)
        self.i += 1
        return self.ps[k], ("ps", k)


def norm_front1(P, nc, name, i, x_src, xt, hb, st_ss, st_ms, st_rs, gb, C, xres=None, q="sp"):
    P.dma(q, xt[i][:], x_src, key=(name, "xt", i), reads=([xres] if xres else []), writes=[(name, "xt", i)])
    P.op("dve", lambda e: e.scalar_tensor_tensor(out=hb[i][:], in0=xt[i][:], scalar=1.0, in1=xt[i][:],
                                                 op0=ALU.mult, op1=ALU.mult, accum_out=st_ss[i][:]),
         reads=[(name, "xt", i)], writes=[(name, "hb", i), ("ss", (name, i))])
    rms_rstd(P, (name, i), st_ss[i][:], st_ms[i][:], st_rs[i][:], C.cneg[:], D)


def norm_front2(P, nc, name, i, xt, hb, st_rs, gb, C):
    P.op("dve", lambda e: e.scalar_tensor_tensor(out=hb[i][:], in0=xt[i][:], scalar=st_rs[i][:], in1=gb[:],
                                                 op0=ALU.mult, op1=ALU.mult),
         reads=[(name, "xt", i), ("rstd", (name, i)), (name, "gb")], writes=[(name, "hb", i)])


def norm_front(P, nc, name, i, x_src, xt, hb, st_ss, st_ms, st_rs, gb, C, xres=None):
    norm_front1(P, nc, name, i, x_src, xt, hb, st_ss, st_ms, st_rs, gb, C, xres=xres)
    norm_front2(P, nc, name, i, xt, hb, st_rs, gb, C)


def norm_back(P, nc, name, i, hb, C, pT, psres):
    for kc in range(NKC):
        P.op("pe", lambda e, kc=kc: e.transpose(out=pT[:, kc * 128:(kc + 1) * 128],
                                                in_=hb[i][:, kc * 128:(kc + 1) * 128], identity=C.ident[:]),
             reads=[(name, "hb", i), "ident"], writes=[psres])


def norm_block(P, nc, name, i, x_src, xt, hb, st_ss, st_ms, st_rs, gb, C, pT, psres):
    norm_front(P, nc, name, i, x_src, xt, hb, st_ss, st_ms, st_rs, gb, C)
    norm_back(P, nc, name, i, hb, C, pT, psres)


def preamble(P, nc, C, ps, x_in, xin_name, g_ap, w0_loads):
    pT = [ps[0][:].bitcast(BF16), ps[1][:].bitcast(BF16)]

    def front(bs):
        for b in bs:
            norm_front1(P, nc, "G", b % 2, x_in[b * 128:(b + 1) * 128, :], C.xt, C.hb, C.ss, C.ms, C.rs, C.gb, C,
                        xres=("xd", xin_name, b), q="act")
        for b in bs:
            norm_front2(P, nc, "G", b % 2, C.xt, C.hb, C.rs, C.gb, C)

    def back(bs):
        for b in bs:
            i = b % 2
            norm_back(P, nc, "G", i, C.hb, C, pT[i], ("ps", i))
            P.op("act", lambda e, i=i, b=b: e.copy(out=C.hT0[:, :, b * 128:(b + 1) * 128],
                                                  in_=pT[i].rearrange("p (k t) -> p k t", k=NKC)),
                 reads=[("ps", i)], writes=[("G", "hT0")])

    def s0():
        P.dma("act", C.gb[:], bcast_rows(g_ap, D), key=("G", "gb"), writes=[("G", "gb")])
        for (dst, src) in w0_loads:
            P.dma("pool", dst, src, key=("G", "W0"), writes=[("G", "W0")])
        front((0, 1))

    def s1():
        back((0, 1))
        front((2, 3))

    def s2():
        back((2, 3))
    return [s0, s1, s2]


def mixer_a(P, nc, st, C, T, x_in, x_mid, W, ps, name="ma", xout_name="xm", next_pre=None):
    nsub = T // 512
    sb = lambda n, shape, dt: st.enter_context(nc.sbuf_tensor(name + n, shape, dt))
    gb = C.gb
    xt = C.xt + [sb("xt%d" % i, [128, D], F32) for i in (2, 3)]
    hb = C.hb + [sb("hb%d" % i, [128, D], BF16) for i in (2, 3)]
    st_ss = C.ss + [sb("ss%d" % i, [128, 1], F32) for i in (2, 3)]
    st_ms = C.ms + [sb("ms%d" % i, [128, 1], F32) for i in (2, 3)]
    st_rs = C.rs + [sb("rs%d" % i, [128, 1], F32) for i in (2, 3)]
    xr = [sb("xr%d" % i, [128, D], F32) for i in range(2)]
    hT = [C.hT0, sb("hT1", [128, NKC, 512], BF16)]
    HRES = [("G", "hT0"), (name, "hT", 1)]
    wina = C.W0[:, 0:NKC * 512].rearrange("p (k n) -> p k n", k=NKC)
    winp = sb("winp", [128, NKC, 256], BF16)
    glu = [sb("glu%d" % i, [128, T + 32], BF16) for i in range(2)]
    PA = [sb("PA%d" % i, [128, T + 16], F32) for i in range(2)]
    F = [sb("F%d" % i, [128, 528], F32) for i in range(4)]
    mixed = [sb("mx%d" % i, [128, 512], BF16) for i in range(4)]
    Dg = sb("Dg", [128, 2, CONV_K, 128], BF16)
    poT = [sb("poT%d" % i, [128, T], BF16) for i in range(2)]
    sgm = [sb("sgm%d" % i, [128, 512], F32) for i in range(2)]
    dwT = sb("dwT", [128, 2, CONV_K], F32)
    dwb = sb("dwb", [128, 2], F32)
    lng = sb("lng", [128, 2], F32)
    lnb = sb("lnb", [128, 2], F32)
    psc = sb("psc", [128, 2], F32)
    pwb = sb("pwb", [128, 2, 256], BF16)
    wblk = sb("wblk", [128, 2, 128], BF16)
    woa = sb("woa", [128, 4, D], BF16)
    onesm = sb("onesm", [128, 128], F32)
    rcL = sb("rcL", [128, 2, 8], F32)
    rcR = sb("rcR", [128, 2, 8], F32)
    tmpE = sb("tmpE", [128, 16], F32)
    co2 = [[sb("co%d_%d" % (q, i), [128, 512], F32) for i in range(2)] for q in range(2)]
    sq = [sb("sq%d" % i, [128, 512], F32) for i in range(2)]
    m2 = sb("m2", [128, 512], F32)
    var = sb("var", [128, 512], F32)
    rstd = sb("rstd", [128, 512], F32)
    t1 = [sb("t1%d" % i, [128, 512], F32) for i in range(2)]
    actb = [sb("act%d" % i, [128, 512], BF16) for i in range(2)]
    coT = sb("coT", [128, 2, 512], BF16)

    pT = [ps[0][:].bitcast(BF16), ps[1][:].bitcast(BF16)]
    bk = Banks(ps)

    P.dma("pool", winp[:], W["w_in"].rearrange("(kc p) n -> p kc n", p=128)[:, :, 512:768], key=(name, "winp"),
          writes=[(name, "winp")])
    dwraw = sb("dwraw", [CONV_K, 256], BF16)
    P.dma("pool", dwraw[:], W["conv_dw"], key=(name, "dwraw"), writes=[(name, "dwraw")])
    for cc in range(2):
        for tl, src in ((dwb, "conv_dw_b"), (lng, "conv_ln_g"), (lnb, "conv_ln_b"), (psc, "pool_scale")):
            a = W[src]
            P.dma("sp", tl[:, cc:cc + 1], bass.AP(a.tensor, a.offset + cc * 128, [[1, 128], [1, 1]]),
                  key=(name, "par", src, cc), writes=[(name, "par", src, cc)])
    PARS = [(name, "par", src, cc) for src in ("conv_dw_b", "conv_ln_g", "conv_ln_b", "pool_scale") for cc in range(2)]
    P.dma("pool", pwb[:], W["conv_pw"].rearrange("(cc p) n -> p cc n", p=128), key=(name, "par2"),
          writes=[(name, "par2")])
    P.op("dve", lambda e: e.memset(wblk[:], 0.0), writes=[(name, "wblk")])
    pw_ = W["pool_w"]
    for g in range(4):
        pc, h = g // 2, g % 2
        P.dma("pool", wblk[h * 64:(h + 1) * 64, pc, h * 64:(h + 1) * 64], pw_[g], key=(name, "wblk"),
              reads=[(name, "wblk")], writes=[(name, "wblk", g)])
    WBLK = [(name, "wblk", g) for g in range(4)]
    wov = W["w_out"].rearrange("(kc p) n -> p kc n", p=128)
    P.dma("pool", woa[:], wov[:, 0:4, :], key=(name, "woa"), writes=[(name, "woa")])
    P.op("dve", lambda e: e.memset(onesm[:], 1.0 / 256.0), writes=[(name, "onesm")])
    def setup_rc():
        for g, w in enumerate(POOL_WINDOWS):
            pc, h = g // 2, g % 2
            rows = slice(h * 64, (h + 1) * 64)
            P.op("dve", lambda e, rows=rows, pc=pc, w=w: e.memset(rcL[rows, pc, :], 1.0 / w), writes=[(name, "rc")])
            P.op("dve", lambda e, rows=rows, pc=pc, w=w: e.memset(rcR[rows, pc, :], 1.0 / w), writes=[(name, "rc")])
            for t in range(8):
                cl = min(w, t + w // 2)
                if cl != w:
                    P.op("dve", lambda e, rows=rows, pc=pc, t=t, cl=cl: e.memset(rcL[rows, pc, t:t + 1], 1.0 / cl),
                         writes=[(name, "rc")])
                k = 8 - t
                cr = min(w, w // 2 + k)
                if cr != w:
                    P.op("dve", lambda e, rows=rows, pc=pc, t=t, cr=cr: e.memset(rcR[rows, pc, t:t + 1], 1.0 / cr),
                         writes=[(name, "rc")])

    for i in range(2):
        P.op("pool", lambda e, i=i: e.memset(glu[i][:, 0:16], 0.0), writes=[(name, "glu", i)])
        P.op("pool", lambda e, i=i: e.memset(glu[i][:, 16 + T:32 + T], 0.0), writes=[(name, "glu", i)])
        P.op("pool", lambda e, i=i: e.memset(PA[i][:, 0:8], 0.0), writes=[(name, "PA", i)])
        P.op("pool", lambda e, i=i: e.memset(PA[i][:, 8 + T:16 + T], 0.0), writes=[(name, "PA", i)])

    def nfront(b):
        i = b % 4
        norm_front(P, nc, "G", i, x_in[b * 128:(b + 1) * 128, :], xt, hb, st_ss, st_ms, st_rs, gb, C)

    def nback(b):
        i, ip, b4, hs_ = b % 4, b % 2, b % 4, (b // 4) % 2
        norm_back(P, nc, "G", i, hb, C, pT[ip], ("ps", ip))
        P.op("act", lambda e: e.copy(out=hT[hs_][:, :, b4 * 128:(b4 + 1) * 128],
                                     in_=pT[ip].rearrange("p (k t) -> p k t", k=NKC)),
             reads=[("ps", ip)], writes=[HRES[hs_]])

    def setup_dg():
        for cc in range(2):
            bank, res = bk.next()
            tb = bank[:].bitcast(BF16)
            P.op("pe", lambda e, cc=cc, tb=tb: e.transpose(out=tb[:, 0:CONV_K], in_=dwraw[:, cc * 128:(cc + 1) * 128],
                                                          identity=C.ident[0:CONV_K, 0:CONV_K]),
                 reads=[(name, "dwraw"), "ident"], writes=[res])
            P.op("act", lambda e, cc=cc, tb=tb: e.copy(out=dwT[:, cc, :], in_=tb[:, 0:CONV_K]), reads=[res], writes=[(name, "dwT")])
        for cc in range(2):
            for k in range(CONV_K):
                P.op("dve", lambda e, cc=cc, k=k: e.tensor_scalar(out=Dg[:, cc, k, :], in0=C.ident[:],
                                                                 scalar1=dwT[:, cc, k:k + 1], scalar2=None, op0=ALU.mult),
                     reads=["ident", (name, "dwT")], writes=[(name, "Dg")])

    for s in range(nsub):
        hs = s % 2
        nb0 = (s + 1) * 4
        more = s + 1 < nsub

        def proj(col0):
            bank, res = bk.next()
            for kc in range(NKC):
                wsrc, wres, cc0 = (wina, ("G", "W0"), col0) if col0 < 512 else (winp, (name, "winp"), col0 - 512)
                P.op("pe", lambda e, kc=kc, bank=bank, cc0=cc0, hs=hs, wsrc=wsrc: e.matmul(
                    bank[:], lhsT=wsrc[:, kc, cc0:cc0 + 128], rhs=hT[hs][:, kc, :],
                    start=(kc == 0), stop=(kc == NKC - 1)),
                    reads=[wres, HRES[hs]], writes=[res])
            return bank, res
        SCH = {2: [("b", 0)], 3: [("b", 1)], 4: [("b", 2)], 5: [("b", 3)]}
        tcount = [0]

        def hook():
            if more:
                for kind, bb in SCH.get(tcount[0], ()):
                    (nfront if kind == "f" else nback)(nb0 + bb)
            tcount[0] += 1
        if more:
            for bb in range(4):
                norm_front1(P, nc, "G", bb, x_in[(nb0 + bb) * 128:(nb0 + bb + 1) * 128, :], xt, hb, st_ss, st_ms, st_rs, gb, C)
            for bb in range(4):
                norm_front2(P, nc, "G", bb, xt, hb, st_rs, gb, C)
        for cc in range(2):
            A, ra = proj(cc * 128)
            hook()
            G, rg = proj(256 + cc * 128)
            P.op("act", lambda e, G=G, cc=cc: e.activation(out=sgm[cc][:], in_=G[:], func=AF.Sigmoid),
                 reads=[rg], writes=[(name, "sgm", cc)])
            P.op("dve", lambda e, A=A, cc=cc, s=s: e.tensor_tensor(out=glu[cc][:, 16 + s * 512:16 + (s + 1) * 512],
                                                                  in0=sgm[cc][:], in1=A[:], op=ALU.mult),
                 reads=[ra, (name, "sgm", cc)], writes=[(name, "glu", cc)])
            hook()
        for pc in range(2):
            Pp, rp = proj(512 + pc * 128)
            P.op("act", lambda e, Pp=Pp, pc=pc, s=s: e.copy(out=PA[pc][:, 8 + s * 512:8 + (s + 1) * 512], in_=Pp[:]),
                 reads=[rp], writes=[(name, "PA", pc)])
            hook()
        if s == 0:
            setup_rc()
        if s == min(1, nsub - 1):
            setup_dg()

    npdone = 0
    if next_pre is not None:
        next_pre[0]()
    rot = [0]

    def rbank():
        k = (0, 1, 6, 7)[rot[0] % 4]
        rot[0] += 1
        return ps[k], ("ps", k)

    def pool_dve(s):
        a = s * 512
        par = s % 2
        for pc in range(2):
            p = PA[pc]
            rp = (name, "PA", pc)
            F0, F1 = F[2 * par], F[2 * par + 1]
            r0, r1 = (name, "F", 2 * par), (name, "F", 2 * par + 1)
            mx = mixed[2 * pc + par]
            rmx = (name, "mx", 2 * pc + par)
            P.op("pool", lambda e, p=p, F0=F0: e.tensor_tensor(out=F0[:, 1:528], in0=p[:, a:a + 527], in1=p[:, a + 1:a + 528], op=ALU.add),
                 reads=[rp], writes=[r0])
            P.op("pool", lambda e, F0=F0, F1=F1: e.tensor_tensor(out=F1[:, 2:526], in0=F0[:, 1:525], in1=F0[:, 3:527], op=ALU.add),
                 reads=[r0], writes=[r1])
            if pc == 1:
                P.op("pool", lambda e, F0=F0, F1=F1: e.tensor_tensor(out=F0[:, 4:524], in0=F1[:, 2:522], in1=F1[:, 6:526], op=ALU.add),
                     reads=[r1], writes=[r0])
                P.op("pool", lambda e, F0=F0, F1=F1: e.tensor_tensor(out=F1[:, 8:520], in0=F0[:, 4:516], in1=F0[:, 12:524], op=ALU.add),
                     reads=[r0], writes=[r1])
            for h in range(2):
                w = POOL_WINDOWS[pc * 2 + h]
                S, rS = (F0, r0) if h == 0 else (F1, r1)
                rows = slice(h * 64, (h + 1) * 64)
                P.op("dve", lambda e, S=S, rows=rows, w=w, p=p, mx=mx: e.scalar_tensor_tensor(
                    out=mx[rows, :], in0=S[rows, 8:520], scalar=1.0 / w, in1=p[rows, 8 + a:8 + a + 512],
                    op0=ALU.mult, op1=ALU.subtract),
                    reads=[rS, rp], writes=[rmx])
                edges = []
                if s == 0:
                    edges.append((0, rcL, 0))
                if s == nsub - 1:
                    edges.append((504, rcR, 8))
                for (c0, rc, e0) in edges:
                    P.op("dve", lambda e, S=S, rows=rows, c0=c0, rc=rc, e0=e0, pc=pc: e.tensor_tensor(
                        out=tmpE[rows, e0:e0 + 8], in0=S[rows, 8 + c0:16 + c0], in1=rc[rows, pc, :], op=ALU.mult),
                        reads=[rS, (name, "rc")], writes=[(name, "tmpE")])
                    P.op("dve", lambda e, rows=rows, c0=c0, e0=e0, p=p, mx=mx: e.tensor_tensor(
                        out=mx[rows, c0:c0 + 8], in0=tmpE[rows, e0:e0 + 8], in1=p[rows, 8 + a + c0:16 + a + c0], op=ALU.subtract),
                        reads=[(name, "tmpE"), rp], writes=[rmx])

    def pool_mm(s):
        par = s % 2
        for pc in range(2):
            bank, res = rbank()
            mx = mixed[2 * pc + par]
            P.op("pe", lambda e, bank=bank, pc=pc, mx=mx: e.matmul(bank[:], lhsT=wblk[:, pc, :], rhs=mx[:], start=True, stop=True),
                 reads=WBLK + [(name, "mx", 2 * pc + par)], writes=[res])
            P.op("dve", lambda e, bank=bank, pc=pc: e.tensor_scalar(
                out=poT[pc][:, s * 512:(s + 1) * 512], in0=bank[:], scalar1=psc[:, pc:pc + 1], scalar2=None, op0=ALU.mult),
                reads=[res] + PARS, writes=[(name, "poT", pc)])

    F32R = mybir.dt.float32r

    def conv_mm(s):
        co = co2[s % 2]
        for cc in range(2):
            bank, res = ps[2 + cc], ("ps", 2 + cc)
            for k in range(CONV_K):
                P.op("pe", lambda e, bank=bank, cc=cc, k=k: e.matmul(
                    bank[:], lhsT=Dg[:, cc, k, :], rhs=glu[cc][:, 1 + s * 512 + k:1 + s * 512 + k + 512],
                    start=(k == 0), stop=(k == CONV_K - 1)),
                    reads=[(name, "Dg"), (name, "glu", cc)], writes=[res])
            P.op("act", lambda e, bank=bank, cc=cc: e.activation(out=co[cc][:], in_=bank[:], func=AF.Identity,
                                                               bias=dwb[:, cc:cc + 1]),
                 reads=[res] + PARS, writes=[(name, "co", s % 2, cc)])
            P.op("act", lambda e, bank=bank, cc=cc: e.activation(out=sq[cc][:], in_=bank[:], func=AF.Square,
                                                               bias=dwb[:, cc:cc + 1]),
                 reads=[res] + PARS, writes=[(name, "sq", cc)])

    def ln_pw(s):
        co = co2[s % 2]
        M, rm = ps[4], ("ps", 4)
        Q, rq = ps[5], ("ps", 5)
        for cc in range(2):
            P.op("pe", lambda e, cc=cc: e.matmul(M[:], lhsT=onesm[:], rhs=co[cc][:],
                                                 start=(cc == 0), stop=(cc == 1)),
                 reads=[(name, "onesm"), (name, "co", s % 2, cc)], writes=[rm])
        for cc in range(2):
            P.op("pe", lambda e, cc=cc: e.matmul(Q[:], lhsT=onesm[:], rhs=sq[cc][:],
                                                 start=(cc == 0), stop=(cc == 1)),
                 reads=[(name, "onesm"), (name, "sq", cc)], writes=[rq])
        P.op("act", lambda e: e.activation(out=m2[:], in_=M[:], func=AF.Square), reads=[rm], writes=[(name, "m2")])
        P.op("dve", lambda e: e.tensor_tensor(out=var[:], in0=Q[:], in1=m2[:], op=ALU.subtract),
             reads=[rq, (name, "m2")], writes=[(name, "var")])
        P.op("act", lambda e: e.activation(out=var[:], in_=var[:], func=AF.Ln, bias=C.ceps[:]),
             reads=[(name, "var"), "ceps"], writes=[(name, "var")])
        P.op("act", lambda e: e.activation(out=rstd[:], in_=var[:], func=AF.Exp, scale=-0.5), reads=[(name, "var")], writes=[(name, "rstd")])

    def ln_b(s):
        co = co2[s % 2]
        M, rm = ps[4], ("ps", 4)
        for cc in range(2):
            P.op("dve", lambda e, cc=cc: e.tensor_tensor(out=t1[cc][:], in0=co[cc][:], in1=M[:], op=ALU.subtract),
                 reads=[(name, "co", s % 2, cc), rm], writes=[(name, "t1", cc)])
            P.op("dve", lambda e, cc=cc: e.tensor_tensor(out=t1[cc][:], in0=t1[cc][:], in1=rstd[:], op=ALU.mult),
                 reads=[(name, "t1", cc), (name, "rstd")], writes=[(name, "t1", cc)])
            P.op("act", lambda e, cc=cc: e.activation(out=actb[cc][:], in_=t1[cc][:], func=AF.Silu,
                                                     bias=lnb[:, cc:cc + 1], scale=lng[:, cc:cc + 1]),
                 reads=[(name, "t1", cc)] + PARS, writes=[(name, "act", cc)])

    def pw_mm(s):
        for oc in range(2):
            bank, res = rbank()
            for cc in range(2):
                P.op("pe", lambda e, bank=bank, oc=oc, cc=cc: e.matmul(
                    bank[:], lhsT=pwb[:, cc, oc * 128:(oc + 1) * 128], rhs=actb[cc][:], start=(cc == 0), stop=(cc == 1)),
                    reads=[(name, "par2"), (name, "act", cc)], writes=[res])
            P.op("act", lambda e, bank=bank, oc=oc: e.copy(out=coT[:, oc, :], in_=bank[:]),
                 reads=[res], writes=[(name, "coT")])

    xr4 = [xr[0], xr[1], xt[2], xt[3]]
    xrr = [(name, "xr", 0), (name, "xr", 1), ("G", "xt", 2), ("G", "xt", 3)]

    def wout(s, blocks=(0, 1, 2, 3)):
        for b4 in blocks:
            b = s * 4 + b4
            i = b % 4
            P.dma("sp", xr4[i][:], x_in[b * 128:(b + 1) * 128, :], key=(name, "xr", i), writes=[xrr[i]])
            for hf in range(2):
                Y, ry = rbank()
                for kc in range(4):
                    if kc < 2:
                        lhs = coT[:, kc, b4 * 128:(b4 + 1) * 128]
                        rd = (name, "coT")
                    else:
                        lhs = poT[kc - 2][:, b * 128:(b + 1) * 128]
                        rd = (name, "poT", kc - 2)
                    P.op("pe", lambda e, Y=Y, lhs=lhs, kc=kc, hf=hf: e.matmul(
                        Y[:], lhsT=lhs, rhs=woa[:, kc, hf * 512:(hf + 1) * 512], start=(kc == 0), stop=(kc == 3)),
                        reads=[rd, (name, "woa")], writes=[ry])
                P.op("dve", lambda e, Y=Y, i=i, hf=hf: e.tensor_tensor(
                    out=xr4[i][:, hf * 512:(hf + 1) * 512], in0=xr4[i][:, hf * 512:(hf + 1) * 512], in1=Y[:], op=ALU.add),
                    reads=[ry, xrr[i]], writes=[xrr[i]])
            o = P.dma("sp", x_mid[b * 128:(b + 1) * 128, :], xr4[i][:], key=(name, "xo", i),
                      reads=[xrr[i]], writes=[("xd", xout_name, b)])
            P.out_ops.append(o)

    pool_dve(0)
    conv_mm(0)
    for s in range(nsub):
        if s >= 1:
            wout(s - 1, (0, 1))
        ln_pw(s)
        ln_b(s)
        if s + 1 < nsub:
            pool_dve(s + 1)
            conv_mm(s + 1)
        if s >= 1:
            wout(s - 1, (2, 3))
        pool_mm(s)
        pw_mm(s)
        if next_pre is not None and s in (1, 3):
            next_pre[(s + 1) // 2]()
            npdone = (s + 1) // 2
    if next_pre is not None:
        for k_ in range(npdone + 1, 3):
            next_pre[k_]()
    wout(nsub - 1)


GW = 64
NEG = -30000.0


def attn_plan(rows):
    nt = rows // 2
    r0 = lambda r: min(max(r - 4, 0), rows - 8)
    variants = []
    pairs = {}

    def key_of(qt, kt):
        valid = tuple(tuple(r0(2 * qt + i) <= 2 * kt + j < r0(2 * qt + i) + 8 for j in range(2)) for i in range(2))
        if not any(any(v) for v in valid):
            return None
        return (2 * kt - 2 * qt, valid)
    km = nt // 2
    order = [(qt, km) for qt in range(nt)] + [(qt, kt) for qt in range(nt) for kt in range(nt)]
    for qt, kt in order:
        key = key_of(qt, kt)
        if key is None:
            continue
        if key not in variants:
            variants.append(key)
        pairs[(qt, kt)] = variants.index(key)
    return nt, variants, pairs


def build_bias_table(rpb, rows):
    nt, variants, pairs = attn_plan(rows)
    H = rpb.shape[0]
    qc = np.arange(GW)
    c0 = np.clip(qc - 8, 0, GW - 16)
    kc = np.arange(GW)
    colvalid = (kc[:, None] >= c0[None, :]) & (kc[:, None] < c0[None, :] + 16)
    dc = np.clip(kc[:, None] - qc[None, :] + 15, 0, 30)
    tab = np.full((H, len(variants), 128, 128), NEG, np.float32)
    for v, (d, valid) in enumerate(variants):
        for i in range(2):
            for j in range(2):
                if not valid[i][j]:
                    continue
                dr = d + j - i + 7
                blk = rpb[:, dr][:, dc]
                tab[:, v, j * 64:(j + 1) * 64, i * 64:(i + 1) * 64] = np.where(colvalid[None], blk, NEG)
    return tab


def mixer_b(P, nc, C, T, x_in, x_mid, x_out, W, btab, ps, name="mb", xout_name="xb", next_pre=None):
    import contextlib
    rows = T // GW
    nt, variants, pairs = attn_plan(rows)
    NV = len(variants)
    nsub = T // 512
    with contextlib.ExitStack() as st:
        sb = lambda n, shape, dt: st.enter_context(nc.sbuf_tensor(name + n, shape, dt))
        qT = sb("qT", [128, 4, T], BF16)
        kT = sb("kT", [128, 4, T], BF16)
        vaug = sb("vaug", [128, nt, 8, 65], BF16)
        gq = sb("gq", [128, 1], F32)
        gk = sb("gk", [128, 1], F32)
        blk64 = sb("blk64", [128, 128], BF16)
        with contextlib.ExitStack() as st2:
            sb2 = lambda n, shape, dt: st2.enter_context(nc.sbuf_tensor(name + n, shape, dt))
            gb, xt, hb, st_ss, st_ms, st_rs = C.gb, C.xt, C.hb, C.ss, C.ms, C.rs
            hT = [C.hT0, sb2("hT1", [128, NKC, 512], BF16)]
            HRES = [("G", "hT0"), (name, "hT", 1)]
            winq = C.W0[:, 0:NKC * 512].rearrange("p (k n) -> p k n", k=NKC)
            winkv = sb2("winkv", [128, NKC, 1024], BF16)
            sqb = [sb2("sqb%d" % i, [128, 512], BF16) for i in range(2)]
            lnv = [sb2("lnv%d" % i, [128, 512], F32) for i in range(2)]
            rsd = [sb2("rsd%d" % i, [128, 512], F32) for i in range(2)]
            pT = [ps[0][:].bitcast(BF16), ps[1][:].bitcast(BF16)]
            bk = Banks(ps)

            winv = W["w_in"].rearrange("(kc p) n -> p kc n", p=128)
            for i in range(2):
                P.dma("pool", winkv[:, :, i * 512:(i + 1) * 512], winv[:, :, 1280 + i * 512:1280 + (i + 1) * 512],
                      key=(name, "winkv", i), writes=[(name, "winkv", i)])
            for tl, src in ((gq, "q_norm"), (gk, "k_norm")):
                a = W[src]
                for h2 in range(2):
                    P.dma("sp", tl[h2 * 64:(h2 + 1) * 64, :], bass.AP(a.tensor, a.offset, [[1, 64], [1, 1]]),
                          key=(name, "par", src), writes=[(name, "par", src, h2)])
            P.op("dve", lambda e: e.tensor_scalar(out=gq[:], in0=gq[:], scalar1=0.125, scalar2=None, op0=ALU.mult),
                 reads=[(name, "par", "q_norm", 0), (name, "par", "q_norm", 1)], writes=[(name, "gq8")])
            P.op("pool", lambda e: e.memset(blk64[:], 0.0), writes=[(name, "blk64")])
            for h2 in range(2):
                P.op("pool", lambda e, h2=h2: e.memset(blk64[h2 * 64:(h2 + 1) * 64, h2 * 64:(h2 + 1) * 64], 1.0 / 64.0),
                     writes=[(name, "blk64")])
            P.op("pool", lambda e: e.memset(vaug[:, :, :, 64:65], 1.0), writes=[(name, "vaug")])
            def nfront(b):
                i = b % 2
                norm_front(P, nc, "G", i, x_in[b * 128:(b + 1) * 128, :], xt, hb, st_ss, st_ms, st_rs, gb, C)

            def nback(b):
                i, b4, hs_ = b % 2, b % 4, (b // 4) % 2
                norm_back(P, nc, "G", i, hb, C, pT[i], ("ps", i))
                P.op("act", lambda e: e.copy(out=hT[hs_][:, :, b4 * 128:(b4 + 1) * 128],
                                             in_=pT[i].rearrange("p (k t) -> p k t", k=NKC)),
                     reads=[("ps", i)], writes=[HRES[hs_]])

            pend = None

            def finish(pd):
                (Qp, rq, par, qk, j, s_) = pd
                SS, rs = bk.next()
                P.op("pe", lambda e: e.matmul(SS[:], lhsT=blk64[:], rhs=sqb[par][:], start=True, stop=True),
                     reads=[(name, "blk64"), (name, "sqb", par)], writes=[rs])
                P.op("act", lambda e: e.activation(out=lnv[par][:], in_=SS[:], func=AF.Ln, bias=C.ceps[:]),
                     reads=[rs, "ceps"], writes=[(name, "lnv", par)])
                P.op("act", lambda e: e.activation(out=rsd[par][:], in_=lnv[par][:], func=AF.Exp, scale=-0.5),
                     reads=[(name, "lnv", par)], writes=[(name, "rsd", par)])
                cols = slice(s_ * 512, (s_ + 1) * 512)
                if qk == 1:
                    P.op("dve", lambda e: e.scalar_tensor_tensor(
                        out=kT[:, j, cols], in0=Qp[:], scalar=gk[:, 0:1], in1=rsd[par][:], op0=ALU.mult, op1=ALU.mult),
                        reads=[rq, (name, "rsd", par), (name, "par", "k_norm", 0), (name, "par", "k_norm", 1)], writes=[(name, "kT", j)])
                else:
                    P.op("dve", lambda e: e.scalar_tensor_tensor(
                        out=qT[:, j, cols], in0=Qp[:], scalar=gq[:, 0:1], in1=rsd[par][:], op0=ALU.mult, op1=ALU.mult),
                        reads=[rq, (name, "rsd", par), (name, "gq8")], writes=[(name, "qT", j)])

            SCHED = {0: [("f", 0), ("f", 1)], 3: [("b", 0)], 4: [("f", 2)], 6: [("b", 1)], 7: [("f", 3)],
                     9: [("b", 2)], 11: [("b", 3)]}

            def hook(s_, t_):
                if s_ + 1 >= nsub:
                    return
                for kind, bb in SCHED.get(t_, ()):
                    (nfront if kind == "f" else nback)((s_ + 1) * 4 + bb)

            cidx = 0
            for s in range(nsub):
                hs = s % 2
                ntask = 0
                for qk in (0, 1):
                    for j in range(4):
                        par = cidx % 2
                        cidx += 1
                        wsrc = winq if qk == 0 else winkv
                        wres = ("G", "W0") if qk == 0 else (name, "winkv", 0)
                        col0 = j * 128
                        Qp, rq = bk.next()
                        for kc in range(NKC):
                            if kc == 2 and pend is not None:
                                finish(pend)
                                pend = None
                            P.op("pe", lambda e, kc=kc, Qp=Qp, col0=col0, hs=hs, wsrc=wsrc: e.matmul(
                                Qp[:], lhsT=wsrc[:, kc, col0:col0 + 128], rhs=hT[hs][:, kc, :],
                                start=(kc == 0), stop=(kc == NKC - 1)),
                                reads=[wres, HRES[hs]], writes=[rq])
                        P.op("act", lambda e, Qp=Qp, par=par: e.activation(out=sqb[par][:], in_=Qp[:], func=AF.Square),
                             reads=[rq], writes=[(name, "sqb", par)])
                        pend = (Qp, rq, par, qk, j, s)
                        hook(s, ntask)
                        ntask += 1
                for b4 in range(4):
                    b = s * 4 + b4
                    Vp, rv = bk.next()
                    for kc in range(NKC):
                        if kc == 2 and pend is not None:
                            finish(pend)
                            pend = None
                        P.op("pe", lambda e, kc=kc, Vp=Vp, b4=b4, hs=hs: e.matmul(
                            Vp[:], lhsT=hT[hs][:, kc, b4 * 128:(b4 + 1) * 128], rhs=winkv[:, kc, 512:1024],
                            start=(kc == 0), stop=(kc == NKC - 1)),
                            reads=[(name, "winkv", 1), HRES[hs]], writes=[rv])
                    P.op("act", lambda e, Vp=Vp, b=b: e.copy(out=vaug[:, b, :, 0:64],
                                                            in_=Vp[:].rearrange("p (h d) -> p h d", h=8)),
                         reads=[rv], writes=[(name, "vaug")])
                    hook(s, 8 + b4)
        P.barrier()
        with contextlib.ExitStack() as st3:
            sb3 = lambda n, shape, dt: st3.enter_context(nc.sbuf_tensor(name + n, shape, dt))
            atok = sb3("atok", [128, nt, 512], BF16)
            bt = [sb3("bt%d" % i, [128, NV, 128], BF16) for i in range(2)]
            NPT = 8
            PT = [sb3("PT%d" % i, [128, 768], BF16) for i in range(NPT)]
            wob = sb3("wob", [128, 4, D], BF16)
            NXR = 3
            xr = [sb3("xr%d" % i, [128, D], F32) for i in range(NXR)]
            rden = [sb3("rden%d" % i, [128, 1], F32) for i in range(4)]
            aTt = [sb3("aTt%d" % i, [128, 4, 128], BF16) for i in range(2)]
            NKZ = 4
            kz = [[sb3("kz%d_%d" % (h2, i), [128, 128], BF16) for i in range(NKZ)] for h2 in range(2)]
            for h2 in range(2):
                for i in range(NKZ):
                    P.op("pool", lambda e, h2=h2, i=i: e.memset(kz[h2][i][:], 0.0), writes=[(name, "kz", h2, i)])

            wov = W["w_out"].rearrange("(kc p) n -> p kc n", p=128)
            P.dma("pool", wob[:], wov[:, 4:8, :], key=(name, "wob"), writes=[(name, "wob")])
            btv = btab.rearrange("h v k q -> h k v q")
            qts_of = {kt: sorted(q for (q, k) in pairs if k == kt) for kt in range(nt)}
            kts_of = {qt: sorted(k for (q, k) in pairs if q == qt) for qt in range(nt)}
            sbanks = [(2, 3), (4, 5)]
            oslot = [0]
            sidx = 0

            def pv(h, qt):
                kts = kts_of[qt]
                ob = (0, 1, 6, 7)[oslot[0] % 4]
                rdi = oslot[0] % 4
                oslot[0] += 1
                O = ps[ob][:, 0:65]
                ro = ("ps", ob)
                for n_, k2 in enumerate(kts):
                    off = (qt - qts_of[k2][0]) * 128
                    P.op("pe", lambda e, k2=k2, off=off, n_=n_: e.matmul(
                        O, lhsT=PT[k2 % NPT][:, off:off + 128], rhs=vaug[:, k2, h, :],
                        start=(n_ == 0), stop=(n_ == len(kts) - 1)),
                        reads=[(name, "PT", k2 % NPT), (name, "vaug")], writes=[ro])
                rd = rden[rdi]
                P.op("dve", lambda e: e.reciprocal(out=rd[:], in_=O[:, 64:65]),
                     reads=[ro], writes=[(name, "rden", rdi)])
                P.op("dve", lambda e: e.tensor_scalar(
                    out=atok[:, qt, h * 64:(h + 1) * 64], in0=O[:, 0:64], scalar1=rd[:, 0:1], scalar2=None, op0=ALU.mult),
                    reads=[ro, (name, "rden", rdi)], writes=[(name, "atok", qt)])

            for h in range(8):
                j, h2 = h // 2, h % 2
                bsl = h % 2
                P.dma("pool", bt[bsl][:], btv[h], key=(name, "bt", bsl), writes=[(name, "bt", bsl)])
                todo = []
                for kt in range(nt):
                    qts = qts_of[kt]
                    qa, qb = qts[0], qts[-1]
                    assert qts == list(range(qa, qb + 1))
                    N = (qb - qa + 1) * 128
                    bA, bB = sbanks[sidx % 2]
                    sidx += 1
                    pslot = kt % NPT
                    segs = [(0, min(N, 512), bA)] + ([(512, N, bB)] if N > 512 else [])
                    kzi = kt % NKZ
                    kzt = kz[h2][kzi]
                    hr = slice(h2 * 64, (h2 + 1) * 64)
                    P.op("pool", lambda e, kzt=kzt, hr=hr, kt=kt, j=j: e.tensor_copy(
                        out=kzt[hr, :], in_=kT[hr, j, kt * 128:(kt + 1) * 128]),
                        reads=[(name, "kT", j)], writes=[(name, "kz", h2, kzi)])
                    for (c0, c1, bnk) in segs:
                        P.op("pe", lambda e, c0=c0, c1=c1, bnk=bnk, kzt=kzt, qa=qa, j=j: e.matmul(
                            ps[bnk][:, 0:c1 - c0], lhsT=kzt[:],
                            rhs=qT[:, j, qa * 128 + c0:qa * 128 + c1], start=True, stop=False),
                            reads=[(name, "qT", j), (name, "kz", h2, kzi)], writes=[("ps", bnk)])
                        tis = list(range(c0 // 128, c1 // 128))
                        runs = []
                        for ti in tis:
                            v = pairs[(qa + ti, kt)]
                            if runs and runs[-1][1] + runs[-1][2] == v and runs[-1][0] + runs[-1][2] == ti:
                                runs[-1][2] += 1
                            else:
                                runs.append([ti, v, 1])
                        for ri, (ti, v, n) in enumerate(runs):
                            P.op("pe", lambda e, ti=ti, v=v, n=n, c0=c0, bnk=bnk, bsl=bsl, last=(ri == len(runs) - 1): e.matmul(
                                ps[bnk][:, ti * 128 - c0:(ti + n) * 128 - c0], lhsT=C.ident[:],
                                rhs=bt[bsl][:, v:v + n, :], start=False, stop=last),
                                reads=["ident", (name, "bt", bsl)], writes=[("ps", bnk)])
                        P.op("act", lambda e, c0=c0, c1=c1, bnk=bnk, pslot=pslot: e.activation(
                            out=PT[pslot][:, c0:c1], in_=ps[bnk][:, 0:c1 - c0], func=AF.Exp),
                            reads=[("ps", bnk)], writes=[(name, "PT", pslot)])
                    for qt in todo:
                        pv(h, qt)
                    todo = [qt for qt in range(nt) if kts_of[qt][-1] == kt]
                for qt in todo:
                    pv(h, qt)
            pT = [ps[0][:].bitcast(BF16), ps[1][:].bitcast(BF16)]
            ybank = [0]

            def front(b):
                i = b % 2
                P.dma("sp", xr[b % NXR][:], x_mid[b * 128:(b + 1) * 128, :], key=(name, "xr", b % NXR),
                      writes=[(name, "xr", b % NXR)])
                for c in range(4):
                    P.op("pe", lambda e, c=c: e.transpose(out=pT[i][:, c * 128:(c + 1) * 128],
                                                          in_=atok[:, b, c * 128:(c + 1) * 128], identity=C.ident[:]),
                         reads=[(name, "atok", b), "ident"], writes=[("ps", i)])
                P.op("act", lambda e: e.copy(out=aTt[i][:], in_=pT[i][:, 0:512].rearrange("p (k t) -> p k t", k=4)),
                     reads=[("ps", i)], writes=[(name, "aTt", i)])

            def back(b):
                i = b % 2
                xi = b % NXR
                for hf in range(2):
                    yb = 2 + (ybank[0] % 6)
                    ybank[0] += 1
                    for kc in range(4):
                        P.op("pe", lambda e, yb=yb, kc=kc, hf=hf: e.matmul(
                            ps[yb][:], lhsT=aTt[i][:, kc, :], rhs=wob[:, kc, hf * 512:(hf + 1) * 512],
                            start=(kc == 0), stop=(kc == 3)),
                            reads=[(name, "aTt", i), (name, "wob")], writes=[("ps", yb)])
                    P.op("dve", lambda e, yb=yb, hf=hf: e.tensor_tensor(
                        out=xr[xi][:, hf * 512:(hf + 1) * 512], in0=xr[xi][:, hf * 512:(hf + 1) * 512], in1=ps[yb][:], op=ALU.add),
                        reads=[("ps", yb), (name, "xr", xi)], writes=[(name, "xr", xi)])
                o = P.dma("sp", x_out[b * 128:(b + 1) * 128, :], xr[xi][:], key=(name, "xo", xi),
                          reads=[(name, "xr", xi)], writes=[("xd", xout_name, b)])
                P.out_ops.append(o)

            front(0)
            for b in range(nt):
                if b + 1 < nt:
                    front(b + 1)
                back(b)
                if next_pre is not None:
                    for k_, bb in enumerate((3, min(8, nt - 2), min(13, nt - 1))):
                        if b == bb:
                            next_pre[k_]()


T_SEQ = 4096
DEPTH = 2
PSHAPES = dict(
    ffn1_norm=[D], ffn1_gate=[D, DFF], ffn1_up=[D, DFF], ffn1_down=[DFF, D], mix_norm=[D], w_in=[D, 2304],
    conv_dw=[31, 256], conv_dw_b=[256], conv_ln_g=[256], conv_ln_b=[256], conv_pw=[256, 256],
    pool_w=[4, 64, 64], pool_scale=[256], q_norm=[64], k_norm=[64], w_out=[1024, 1024],
    ffn2_norm=[D], ffn2_gate=[D, DFF], ffn2_up=[D, DFF], ffn2_down=[DFF, D])


def build_program(L, T=T_SEQ, TT=1024, GRP=2):
    import contextlib
    rows = T // GW
    NV = len(attn_plan(rows)[1])
    nc = bass.Bass("TRN2", target_bir_lowering=False)
    x = nc.dram_tensor("x", [T, D], F32, kind="ExternalInput").ap()
    Wf = {k: nc.dram_tensor(k, [L] + v, F32, kind="ExternalInput").ap() for k, v in PSHAPES.items()}
    btab = nc.dram_tensor("btab", [L, 8, NV, 128, 128], F32, kind="ExternalInput").ap()
    y = nc.dram_tensor("y", [T, D], F32, kind="ExternalOutput").ap()
    scr = [nc.dram_tensor("scr%d" % i, [T, D], F32).ap() for i in range(4)]
    with contextlib.ExitStack() as st:
        ps = [st.enter_context(nc.psum_tensor("ps%d" % i, [128, 512], F32)) for i in range(8)]
        C = Consts(nc, st)
        P = Prog(nc)
        C.init(P)
        def W0v(n):
            return C.W0[:, 0:NKC * n].rearrange("p (k n) -> p k n", k=NKC)

        def pre_ffn(x_ap, xname, g_ap, wg, wu):
            v = C.W0[:, 0:NKC * 2 * 256].rearrange("p (k w c) -> p k w c", k=NKC, w=2)
            wgv = wg.rearrange("(kc p) n -> p kc n", p=128)
            wuv = wu.rearrange("(kc p) n -> p kc n", p=128)
            loads = [(v[:, :, 0, :], wgv[:, :, 0:256]), (v[:, :, 1, :], wuv[:, :, 0:256])]
            return preamble(P, nc, C, ps, x_ap, xname, g_ap, loads)

        def pre_mix(x_ap, xname, W, c0, n):
            winv = W["w_in"].rearrange("(kc p) n -> p kc n", p=128)
            loads = [(W0v(n), winv[:, :, c0:c0 + n])]
            return preamble(P, nc, C, ps, x_ap, xname, W["mix_norm"], loads)

        phases = []
        cur, curname = x, "xin"
        for l in range(L):
            W = {k: v[l] for k, v in Wf.items()}
            xo, xoname = (y, "y") if l == L - 1 else (scr[3], "xo%d" % l)
            phases.append(dict(kind="f1", l=l, W=W, xin=cur, xin_name=curname, xout=scr[0], xout_name="xa%d" % l))
            phases.append(dict(kind="ma", l=l, W=W, xin=scr[0], xin_name="xa%d" % l, xout=scr[1], xout_name="xm%d" % l))
            phases.append(dict(kind="mb", l=l, W=W, xin=scr[0], xin_name="xa%d" % l, xmid=scr[1], xout=scr[2], xout_name="xb%d" % l))
            phases.append(dict(kind="f2", l=l, W=W, xin=scr[2], xin_name="xb%d" % l, xout=xo, xout_name=xoname))
            cur, curname = xo, xoname

        def mk_pre(ph):
            W = ph["W"]
            if ph["kind"] == "f1":
                return pre_ffn(ph["xin"], ph["xin_name"], W["ffn1_norm"], W["ffn1_gate"], W["ffn1_up"])
            if ph["kind"] == "f2":
                return pre_ffn(ph["xin"], ph["xin_name"], W["ffn2_norm"], W["ffn2_gate"], W["ffn2_up"])
            if ph["kind"] == "ma":
                return pre_mix(ph["xin"], ph["xin_name"], W, 0, 512)
            return pre_mix(ph["xin"], ph["xin_name"], W, 768, 512)

        for st_ in mk_pre(phases[0]):
            st_()
        for i, ph in enumerate(phases):
            W, l = ph["W"], ph["l"]
            nxt = mk_pre(phases[i + 1]) if i + 1 < len(phases) else None
            if ph["kind"] in ("f1", "f2"):
                k = "ffn1" if ph["kind"] == "f1" else "ffn2"
                with contextlib.ExitStack() as s1:
                    ffn_phase(P, nc, s1, C, T, ph["xin"], ph["xout"], W[k + "_norm"], W[k + "_gate"], W[k + "_up"],
                              W[k + "_down"], ps, TT=TT, GRP=GRP, name="%s_%d" % (ph["kind"], l),
                              xout_name=ph["xout_name"], next_pre=nxt)
            elif ph["kind"] == "ma":
                with contextlib.ExitStack() as s2:
                    mixer_a(P, nc, s2, C, T, ph["xin"], ph["xout"], W, ps, name="ma%d" % l,
                            xout_name=ph["xout_name"], next_pre=nxt)
            else:
                mixer_b(P, nc, C, T, ph["xin"], ph["xmid"], ph["xout"], W, btab[l], ps, name="mb%d" % l,
                        xout_name=ph["xout_name"], next_pre=nxt)
            P.barrier()
        nblk = T // 128
        P.emit(final_wait_ops=P.out_ops[-nblk:])
    return nc


_PROGS = {}


def _get_prog(L):
    if L not in _PROGS:
        _PROGS[L] = build_program(L)
    return _PROGS[L]


FUSED = True


def kernel(**inputs):
    x = np.ascontiguousarray(np.asarray(inputs["x"], dtype=np.float32))
    B = x.shape[0]
    assert B == 8 and x.shape[1] == T_SEQ and x.shape[2] == D
    Wn = {k: np.ascontiguousarray(np.asarray(inputs[k], dtype=np.float32)) for k in PSHAPES}
    rpb = np.asarray(inputs["rpb"], dtype=np.float32)
    btab = np.stack([build_bias_table(rpb[l], T_SEQ // GW) for l in range(DEPTH)], 0)
    cores = list(range(B))
    if FUSED:
        nc = _get_prog(DEPTH)
        in_maps = [dict(x=x[b], btab=btab, **Wn) for b in cores]
        res = run_bass_kernel_spmd(nc, in_maps, core_ids=cores)
        return np.stack([np.asarray(res.results[b]["y"]) for b in cores], 0).astype(np.float32)
    cur = x
    nc = _get_prog(1)
    for l in range(DEPTH):
        in_maps = [dict(x=cur[b], btab=btab[l:l + 1], **{k: v[l:l + 1] for k, v in Wn.items()}) for b in cores]
        res = run_bass_kernel_spmd(nc, in_maps, core_ids=cores)
        cur = np.stack([np.asarray(res.results[b]["y"]) for b in cores], 0).astype(np.float32)
    return cur
```
